# Optimizing a Trainium2 kernel written in Bass

```python
import jax, jax.numpy as jnp
from jax import lax
import numpy as np

D_MODEL = 1024
BATCH = 8
SEQ = 8192
DEPTH = 2

HEAD_DIM = 64
N_HEADS_A = D_MODEL // (2 * HEAD_DIM)
N_HEADS_B = D_MODEL // (2 * HEAD_DIM)
N_KV_B = N_HEADS_B // 4
N_HEADS_C = D_MODEL // HEAD_DIM
WINDOW_B = 128
DILATION_PAIRS = ((128, 1), (512, 4), (2048, 16))
BLOCK = 128
D_FF = 4 * D_MODEL
EPS = 1e-6
EVEN_IN_WIDTH = 3 * N_HEADS_A * HEAD_DIM + N_HEADS_B * HEAD_DIM + 2 * N_KV_B * HEAD_DIM
EVEN_MIX_WIDTH = (N_HEADS_A + N_HEADS_B) * HEAD_DIM
ODD_IN_WIDTH = 3 * N_HEADS_C * HEAD_DIM + N_HEADS_C
ODD_MIX_WIDTH = N_HEADS_C * HEAD_DIM

kernel_name = "hybrid_dilated_swa_sink_fox_sqrelu"


def rms_norm(x, gain):
    xf = x.astype(jnp.float32)
    y = xf * lax.rsqrt(jnp.mean(xf * xf, axis=-1, keepdims=True) + EPS)
    return (y * gain.astype(jnp.float32)).astype(x.dtype)


def alibi_slopes(n):
    return jnp.asarray(2.0 ** (-8.0 * np.arange(1, n + 1) / n), dtype=jnp.float32)


def dilated_attention(q, k, v, slopes):
    bsz, seq, nh, dh = q.shape
    nblk = seq // BLOCK
    scale = dh ** -0.5
    q_blocks = jnp.moveaxis(q.reshape(bsz, nblk, BLOCK, nh, dh), 1, 0)
    slope_b = slopes[None, :, None, None]

    def block_fn(args):
        blk, qb = args
        t = blk * BLOCK + jnp.arange(BLOCK)
        parts = []
        for window, dil in DILATION_PAIRS:
            dist = jnp.arange(window // dil + 1) * dil
            idx = t[:, None] - dist[None, :]
            valid = idx >= 0
            idx = jnp.maximum(idx, 0)
            kg = k[:, idx]
            vg = v[:, idx]
            s = jnp.einsum("bqhd,bqjhd->bhqj", qb, kg).astype(jnp.float32) * scale
            s = s - slope_b * dist.astype(jnp.float32)
            s = jnp.where(valid[None, None], s, -jnp.inf)
            m = jnp.max(s, axis=-1, keepdims=True)
            p = jnp.exp(s - m)
            den = jnp.sum(p, axis=-1, keepdims=True)
            num = jnp.einsum("bhqj,bqjhd->bhqd", p.astype(v.dtype), vg).astype(jnp.float32)
            parts.append((m, num, den))
        m_all = parts[0][0]
        for m, _, _ in parts[1:]:
            m_all = jnp.maximum(m_all, m)
        num_all = 0.0
        den_all = 0.0
        for m, num, den in parts:
            w = jnp.exp(m - m_all)
            num_all = num_all + w * num
            den_all = den_all + w * den
        out = num_all / den_all
        return jnp.transpose(out, (0, 2, 1, 3)).astype(q.dtype)

    out = lax.map(block_fn, (jnp.arange(nblk), q_blocks))
    return jnp.moveaxis(out, 0, 1).reshape(bsz, seq, nh, dh)


def sliding_window_sink_attention(q, k, v, sinks, slopes):
    bsz, seq, nhq, dh = q.shape
    nkv = k.shape[2]
    grp = nhq // nkv
    nblk = seq // BLOCK
    scale = dh ** -0.5
    qb = q.reshape(bsz, nblk, BLOCK, nkv, grp, dh)
    pad = ((0, 0), (BLOCK, 0), (0, 0), (0, 0))
    kp = jnp.pad(k, pad).reshape(bsz, nblk + 1, BLOCK, nkv, dh)
    vp = jnp.pad(v, pad).reshape(bsz, nblk + 1, BLOCK, nkv, dh)
    kw = jnp.concatenate([kp[:, :-1], kp[:, 1:]], axis=2)
    vw = jnp.concatenate([vp[:, :-1], vp[:, 1:]], axis=2)
    s = jnp.einsum("bnqkgd,bnskd->bnkgqs", qb, kw).astype(jnp.float32) * scale
    qpos = jnp.arange(BLOCK)
    kpos = jnp.arange(2 * BLOCK)
    dist = qpos[:, None] - kpos[None, :] + BLOCK
    abs_k = jnp.arange(nblk)[:, None] * BLOCK + kpos[None, :] - BLOCK
    valid = ((dist >= 0) & (dist < WINDOW_B))[None] & (abs_k >= 0)[:, None, :]
    slope_kg = slopes.reshape(nkv, grp)[:, :, None, None]
    s = s - slope_kg * dist.astype(jnp.float32)
    s = jnp.where(valid[None, :, None, None], s, -jnp.inf)
    sink = sinks.astype(jnp.float32).reshape(nkv, grp)[:, :, None, None]
    m = jnp.maximum(jnp.max(s, axis=-1, keepdims=True), sink)
    p = jnp.exp(s - m)
    den = jnp.sum(p, axis=-1) + jnp.exp(sink - m)[..., 0]
    num = jnp.einsum("bnkgqs,bnskd->bnqkgd", p.astype(v.dtype), vw).astype(jnp.float32)
    out = num / jnp.transpose(den, (0, 1, 4, 2, 3))[..., None]
    return out.astype(q.dtype).reshape(bsz, seq, nhq, dh)


def forgetting_attention(q, k, v, log_f):
    bsz, seq, nh, dh = q.shape
    nblk = seq // BLOCK
    scale = dh ** -0.5
    c = jnp.cumsum(log_f, axis=1)
    c_k = jnp.transpose(c, (0, 2, 1))
    q_blocks = jnp.moveaxis(q.reshape(bsz, nblk, BLOCK, nh, dh), 1, 0)
    c_blocks = jnp.moveaxis(c.reshape(bsz, nblk, BLOCK, nh), 1, 0)
    key_pos = jnp.arange(seq)

    def block_fn(args):
        blk, qb, cq = args
        t = blk * BLOCK + jnp.arange(BLOCK)
        s = jnp.einsum("bqhd,bshd->bhqs", qb, k).astype(jnp.float32) * scale
        s = s + jnp.transpose(cq, (0, 2, 1))[..., None] - c_k[:, :, None, :]
        s = jnp.where((key_pos[None, :] <= t[:, None])[None, None], s, -jnp.inf)
        p = jax.nn.softmax(s, axis=-1)
        return jnp.einsum("bhqs,bshd->bqhd", p.astype(v.dtype), v)

    out = lax.map(block_fn, (jnp.arange(nblk), q_blocks, c_blocks))
    return jnp.moveaxis(out, 0, 1).reshape(bsz, seq, nh, dh)


def even_mixer(h, w_in, a_q_gain, a_k_gain, b_q_gain, b_k_gain, b_sinks, w_out):
    bsz, seq, _ = h.shape
    proj = jnp.einsum("bsd,de->bse", h, w_in)
    widths = [N_HEADS_A * HEAD_DIM] * 3 + [N_HEADS_B * HEAD_DIM, N_KV_B * HEAD_DIM, N_KV_B * HEAD_DIM]
    offs = np.cumsum(widths)[:-1].tolist()
    aq, ak, av, bq, bk, bv = jnp.split(proj, offs, axis=-1)
    aq = rms_norm(aq.reshape(bsz, seq, N_HEADS_A, HEAD_DIM), a_q_gain)
    ak = rms_norm(ak.reshape(bsz, seq, N_HEADS_A, HEAD_DIM), a_k_gain)
    av = av.reshape(bsz, seq, N_HEADS_A, HEAD_DIM)
    bq = rms_norm(bq.reshape(bsz, seq, N_HEADS_B, HEAD_DIM), b_q_gain)
    bk = rms_norm(bk.reshape(bsz, seq, N_KV_B, HEAD_DIM), b_k_gain)
    bv = bv.reshape(bsz, seq, N_KV_B, HEAD_DIM)
    slopes = alibi_slopes(N_HEADS_B + N_HEADS_A)
    a_out = dilated_attention(aq, ak, av, slopes[N_HEADS_B:])
    b_out = sliding_window_sink_attention(bq, bk, bv, b_sinks, slopes[:N_HEADS_B])
    y = jnp.concatenate([a_out.reshape(bsz, seq, -1), b_out.reshape(bsz, seq, -1)], axis=-1)
    return jnp.einsum("bse,ed->bsd", y, w_out)


def odd_mixer(h, w_in, b_forget, c_q_gain, c_k_gain, w_out):
    bsz, seq, _ = h.shape
    proj = jnp.einsum("bsd,de->bse", h, w_in)
    w = N_HEADS_C * HEAD_DIM
    q = rms_norm(proj[..., :w].reshape(bsz, seq, N_HEADS_C, HEAD_DIM), c_q_gain)
    k = rms_norm(proj[..., w:2 * w].reshape(bsz, seq, N_HEADS_C, HEAD_DIM), c_k_gain)
    v = proj[..., 2 * w:3 * w].reshape(bsz, seq, N_HEADS_C, HEAD_DIM)
    f_logit = proj[..., 3 * w:].astype(jnp.float32) + b_forget.astype(jnp.float32)
    log_f = jax.nn.log_sigmoid(f_logit)
    y = forgetting_attention(q, k, v, log_f)
    return jnp.einsum("bse,ed->bsd", y.reshape(bsz, seq, w), w_out)


def squared_relu_mlp(h, w_up, w_down):
    u = jnp.einsum("bsd,df->bsf", h, w_up)
    return jnp.einsum("bsf,fd->bsd", jnp.square(jax.nn.relu(u)), w_down)


def setup_inputs(seed: int = 0) -> dict:
    key = jax.random.key(seed)
    ks = jax.random.split(key, 18)
    n_even = (DEPTH + 1) // 2
    n_odd = DEPTH // 2
    d = D_MODEL

    def nrm(k, shape, fan_in):
        return jax.random.normal(k, shape, jnp.float32) * fan_in ** -0.5

    def gain(k, shape):
        return 1.0 + 0.02 * jax.random.normal(k, shape, jnp.float32)

    return {
        "x": jax.random.normal(ks[0], (BATCH, SEQ, d), jnp.float32),
        "g_mix": gain(ks[1], (DEPTH, d)),
        "g_mlp": gain(ks[2], (DEPTH, d)),
        "w_in_even": nrm(ks[3], (n_even, d, EVEN_IN_WIDTH), d),
        "a_q_gain": gain(ks[4], (n_even, HEAD_DIM)),
        "a_k_gain": gain(ks[5], (n_even, HEAD_DIM)),
        "b_q_gain": gain(ks[6], (n_even, HEAD_DIM)),
        "b_k_gain": gain(ks[7], (n_even, HEAD_DIM)),
        "b_sinks": 0.5 * jax.random.normal(ks[8], (n_even, N_HEADS_B), jnp.float32),
        "w_out_even": nrm(ks[9], (n_even, EVEN_MIX_WIDTH, d), EVEN_MIX_WIDTH),
        "w_in_odd": nrm(ks[10], (n_odd, d, ODD_IN_WIDTH), d),
        "b_forget": jax.random.uniform(ks[11], (n_odd, N_HEADS_C), jnp.float32, 2.0, 6.0),
        "c_q_gain": gain(ks[12], (n_odd, HEAD_DIM)),
        "c_k_gain": gain(ks[13], (n_odd, HEAD_DIM)),
        "w_out_odd": nrm(ks[14], (n_odd, ODD_MIX_WIDTH, d), ODD_MIX_WIDTH),
        "w_up": nrm(ks[15], (DEPTH, d, D_FF), d),
        "w_down": nrm(ks[16], (DEPTH, D_FF, d), D_FF),
    }


def reference(x, g_mix, g_mlp, w_in_even, a_q_gain, a_k_gain, b_q_gain, b_k_gain, b_sinks,
              w_out_even, w_in_odd, b_forget, c_q_gain, c_k_gain, w_out_odd, w_up, w_down):
    for layer in range(DEPTH):
        i = layer // 2
        h = rms_norm(x, g_mix[layer])
        if layer % 2 == 0:
            x = x + even_mixer(h, w_in_even[i], a_q_gain[i], a_k_gain[i], b_q_gain[i],
                               b_k_gain[i], b_sinks[i], w_out_even[i])
        else:
            x = x + odd_mixer(h, w_in_odd[i], b_forget[i], c_q_gain[i], c_k_gain[i], w_out_odd[i])
        h = rms_norm(x, g_mlp[layer])
        x = x + squared_relu_mlp(h, w_up[layer], w_down[layer])
    return x
```

```python
import numpy as np
import ml_dtypes
from contextlib import ExitStack
import concourse.bass as bass
import concourse.mybir as mybir
from concourse.bass_utils import run_bass_kernel_spmd

F32 = mybir.dt.float32
BF16 = mybir.dt.bfloat16
AF = mybir.ActivationFunctionType
ALU = mybir.AluOpType
AX = mybir.AxisListType

D = 1024
DFF = 4096
HD = 64
EPS = 1e-6
E0 = 2304
E1 = 3088
NEG = -30000.0


class Buf:
    __slots__ = ("name", "w", "r", "dsem")

    def __init__(self, name, dsem=None):
        self.name = name
        self.w = None
        self.r = []
        self.dsem = dsem


class CountSem:
    LIMIT = 30000

    def __init__(self, nc, name):
        self.nc = nc
        self.name = name
        self.cur = None
        self.count = 0
        self.n = 0

    def bump(self, inc):
        if self.cur is None or self.count + inc > self.LIMIT:
            self.cur = self.nc.alloc_semaphore(name=f"{self.name}_{self.n}")
            self.n += 1
            self.count = 0
        self.count += inc
        return (self.cur, self.count)


class Op:
    __slots__ = ("eng", "meth", "args", "kwargs", "deps", "need", "ticket", "dsem")


class Prog:
    ENG = ("pe", "act", "dve", "pool", "sp", "q2")
    ENGOBJ = {"pe": "tensor", "act": "scalar", "dve": "vector", "pool": "gpsimd", "sp": "sync"}

    def __init__(self, nc):
        self.nc = nc
        self.all = []
        self.sems = {e: CountSem(nc, "s" + e) for e in ("pe", "act", "dve", "pool")}
        self.dma_pool = []
        self.phase_ops = {e: None for e in ("pe", "act", "dve", "pool", "sp")}
        self.phase_dmas = []
        self.pending = {}
        self.chan_last = {}

    def buf(self, name, dma=False):
        self.nb = getattr(self, "nb", 0) + 1
        return Buf(name, CountSem(self.nc, f"d{name}_{self.nb}") if dma else None)

    def chan(self, name):
        b = self.buf(name, dma=True)
        return b

    def _finish(self, o, reads, writes, extra_deps=(), nowaw=False):
        deps = list(extra_deps)
        for b in reads:
            if b.w is not None:
                deps.append(b.w)
        for b in writes:
            if b.w is not None and not (nowaw and b.w.eng == o.eng and b.w.dsem is None):
                deps.append(b.w)
            deps.extend(b.r)
        for b in reads:
            b.r.append(o)
        for b in writes:
            b.w = o
            b.r = []
        pend = self.pending.pop(o.eng, None)
        if pend:
            deps.extend(pend)
        seen = set()
        out = []
        for d in deps:
            if id(d) in seen or d is o:
                continue
            seen.add(id(d))
            if o.eng == "pe" and d.eng == "pe" and d.dsem is None:
                continue
            d.need = True
            out.append(d)
        o.deps = out
        self.all.append(o)
        self.phase_ops[o.eng] = o
        return o

    def op(self, eng, meth, *args, reads=(), writes=(), nowaw=False, **kwargs):
        o = Op()
        o.eng, o.meth, o.args, o.kwargs = eng, meth, args, kwargs
        o.need, o.ticket, o.dsem = False, None, None
        return self._finish(o, reads, writes, (), nowaw)

    def dma(self, eng, out, in_, sem_buf, reads=(), writes=(), serial=False, **kwargs):
        o = Op()
        kw = dict(out=out, in_=in_)
        kw.update(kwargs)
        o.eng, o.meth, o.args, o.kwargs = eng, "dma_start", (), kw
        o.need, o.ticket, o.dsem = True, None, sem_buf.dsem
        assert o.dsem is not None, sem_buf.name
        self.phase_dmas.append(o)
        extra = ()
        if serial:
            prev = self.chan_last.get(id(sem_buf))
            if prev is not None:
                extra = (prev,)
            self.chan_last[id(sem_buf)] = o
        return self._finish(o, reads, writes, extra)

    def barrier(self):
        deps = [o for o in self.phase_ops.values() if o is not None] + self.phase_dmas
        last = {}
        for d in deps:
            if d.dsem is not None:
                last[id(d.dsem)] = d
            else:
                last[d.eng] = d
        deps = list(last.values())
        for e in ("pe", "act", "dve", "pool", "sp"):
            self.pending[e] = list(deps)
        self.phase_dmas = []

    def emit(self, final_waits=()):
        nc = self.nc
        for o in self.all:
            if o.dsem is not None:
                o.ticket = o.dsem.bump(16)
            elif o.need:
                o.ticket = self.sems[o.eng].bump(1)
        per = {e: [] for e in self.ENGOBJ}
        for o in self.all:
            per[o.eng].append(o)
        stats = {}

        def run(e, ename, ops, extra_waits):
            waited = {}
            nw = 0
            for o in ops:
                for d in o.deps:
                    h, v = d.ticket
                    if waited.get(h.num, 0) < v:
                        e.wait_ge(h, v)
                        waited[h.num] = v
                        nw += 1
                ins = getattr(e, o.meth)(*o.args, **o.kwargs)
                if o.ticket is not None:
                    ins.then_inc(o.ticket[0], 16 if o.dsem is not None else 1)
            for d in extra_waits:
                h, v = d.ticket
                if waited.get(h.num, 0) < v:
                    e.wait_ge(h, v)
                    waited[h.num] = v
            stats[ename] = (len(ops), nw)

        with nc.Block() as block:
            for ename in self.ENGOBJ:
                ops = per[ename]
                extra = final_waits if ename == "sp" else ()
                if not ops and not extra:
                    continue
                deco = getattr(block, self.ENGOBJ[ename])

                def body(e, ename=ename, ops=ops, extra=extra):
                    run(e, ename, ops, extra)

                deco(body)
        return stats


def _bf(a):
    return np.asarray(a, dtype=np.float32).astype(ml_dtypes.bfloat16)


def _consts():
    c = {}
    c["ident"] = _bf(np.eye(128))
    slopes = (2.0 ** (-8.0 * np.arange(1, 17) / 16)).astype(np.float32)
    sl_b, sl_a = slopes[:8], slopes[8:]
    k = np.arange(128)[:, None]
    q = np.arange(128)[None, :]
    am = np.zeros((128, 20, 512), np.float32)
    for r in range(20):
        rel = r - 16
        for j in range(4):
            dist = (j - rel) * 128 + q - k
            cnt = ((dist >= 0) & (dist <= 128)).astype(np.float32)
            cnt += ((dist >= 0) & (dist <= 512) & (dist % 4 == 0))
            cnt += ((dist >= 0) & (dist <= 2048) & (dist % 16 == 0))
            am[:, r, j * 128:(j + 1) * 128] = cnt
    c["amask"] = _bf(am)
    ab = np.zeros((128, 8, 20), np.float32)
    for h in range(8):
        for r in range(20):
            ab[:, h, r] = sl_a[h] * ((r - 16) * 128 + np.arange(128) - 511)
    c["abias"] = ab.reshape(128, 160)
    bh = np.zeros((128, 8, 2, 128), np.float32)
    for h in range(8):
        for d in range(2):
            dist = d * 128 + q - k
            valid = (dist >= 0) & (dist < 128)
            bh[:, h, d, :] = np.where(valid, -sl_b[h] * dist, NEG)
    bh = np.ascontiguousarray(bh.reshape(128, 2, 4, 2, 128).transpose(0, 1, 3, 2, 4))
    hi = _bf(bh)
    lo = _bf(bh - hi.astype(np.float32))
    c["bb_hi"] = hi.reshape(128, 8 * 2 * 128)
    c["bb_lo"] = lo.reshape(128, 8 * 2 * 128)
    c["negdiag"] = _bf(np.where(k > q, NEG, 0.0))
    c["ucum"] = -(k <= q).astype(np.float32)
    e = np.zeros((128, 128), np.float32)
    e[127, :] = 1.0
    c["elast"] = e
    return c


CONST_SPECS = [("ident", [128, 128], BF16), ("amask", [128, 20, 512], BF16), ("abias", [128, 160], F32),
               ("bb_hi", [128, 2048], BF16), ("bb_lo", [128, 2048], BF16), ("negdiag", [128, 128], BF16),
               ("ucum", [128, 128], F32), ("elast", [128, 128], F32)]

PARAM_SPECS = [("g_mix_t", [2, 128, 8]), ("g_mlp_t", [2, 128, 8]), ("w_in0", [D, E0]), ("w_out0", [D, D]),
               ("w_in1", [D, E1]), ("w_out1", [D, D]), ("w_up", [2, D, DFF]), ("w_down", [2, DFF, D]),
               ("gains0", [128, 26 * 64]), ("gains1", [128, 32 * 64]), ("sinks_r", [128, 8]), ("bf_r", [128, 16])]


def build(S, debug=False, phases=(1, 2, 3, 4, 5, 6)):
    T = S // 128
    NG = S // 512
    nc = bass.Bass("TRN2", target_bir_lowering=False)
    P = Prog(nc)
    dr = {}
    dr["x"] = nc.dram_tensor("x", [S, D], F32, kind="ExternalInput").ap()
    for n, shp in PARAM_SPECS:
        dr[n] = nc.dram_tensor(n, shp, F32, kind="ExternalInput").ap()
    for n, shp, dt in CONST_SPECS:
        dr[n] = nc.dram_tensor(n, shp, dt, kind="ExternalInput").ap()
    dr["out"] = nc.dram_tensor("out", [S, D], F32, kind="ExternalOutput").ap()
    sk = "ExternalOutput" if debug else "Internal"
    dr["p0"] = nc.dram_tensor("p0", [S, E0], BF16, kind=sk).ap()
    dr["yT0"] = nc.dram_tensor("yT0", [D, S], BF16, kind=sk).ap()
    dr["x2"] = nc.dram_tensor("x2", [S, D], F32, kind=sk).ap()
    dr["p1"] = nc.dram_tensor("p1", [S, 3072], BF16, kind=sk).ap()
    dr["negc"] = nc.dram_tensor("negc", [S, 16], F32, kind=sk).ap()
    dr["caugT"] = nc.dram_tensor("caugT", [48, S], BF16, kind=sk).ap()
    dr["yT1"] = nc.dram_tensor("yT1", [D, S], BF16, kind=sk).ap()

    psum = [nc.alloc_psum_tensor(f"ps{i}", [128, 512], F32) for i in range(8)]
    epsc = nc.alloc_sbuf_tensor("epsc", [128, 1], F32)
    Beps = P.buf("epsc")
    P.op("dve", "memset", epsc[:], EPS, writes=[Beps])
    out_dmas = []

    uid = [0]

    def sbuf(st, name, shape, dt):
        uid[0] += 1
        return st.enter_context(nc.sbuf_tensor(f"{name}_{uid[0]}", shape, dt))

    def load_weight_bf16(st, name, src, rows, cols, g_ap, stage, Bst, dst, Bdst, Bgx):
        nr = rows // 128
        for r in range(nr):
            s = r % 2
            c0 = 0
            while c0 < cols:
                cw = min(4096, cols - c0)
                P.dma("sp", stage[s][:, 0:cw], src[r * 128:(r + 1) * 128, c0:c0 + cw], Bst[s], writes=[Bst[s]])
                if g_ap is not None:
                    P.op("act", "activation", out=dst[:, r, c0:c0 + cw], in_=stage[s][:, 0:cw], func=AF.Copy,
                         scale=g_ap[:, r:r + 1], reads=[Bst[s], Bgx], writes=[Bdst])
                else:
                    P.op("act", "activation", out=dst[:, r, c0:c0 + cw], in_=stage[s][:, 0:cw], func=AF.Copy,
                         reads=[Bst[s]], writes=[Bdst])
                c0 += cw
                s ^= 1

    def rstd_chain(ss, Bss, rs, Brs, n, width):
        P.op("act", "activation", out=rs, in_=ss, func=AF.Sqrt, scale=1.0 / width, bias=epsc[:, 0:1], reads=[Bss, Beps],
             writes=[Brs])
        P.op("dve", "reciprocal", out=rs, in_=rs, reads=[Brs], writes=[Brs])

    def qkv_phase(layer):
        E = E0 if layer == 0 else E1
        xin = dr["x"] if layer == 0 else dr["x2"]
        w_in = dr["w_in0"] if layer == 0 else dr["w_in1"]
        nqk = 26 if layer == 0 else 32
        with ExitStack() as st:
            Wb = sbuf(st, "Wb", [128, 8, E], BF16); BW = P.buf("Wb")
            stage = [sbuf(st, f"wst{i}", [128, E], F32) for i in range(2)]
            Bst = [P.buf(f"wst{i}", dma=True) for i in range(2)]
            chC = P.chan("chC")
            gt = sbuf(st, "gt", [128, 8], F32); Bg = P.buf("gt")
            gains = sbuf(st, "gains", [128, nqk, 64], F32); Bgn = P.buf("gains")
            idt = sbuf(st, "idt", [128, 128], BF16); Bid = P.buf("idt")
            xt = [sbuf(st, f"xt{i}", [128, D], F32) for i in range(2)]
            Bxt = [P.buf(f"xt{i}", dma=True) for i in range(2)]
            def pair(name, shape, dt, n=2):
                return [sbuf(st, f"{name}{i}", shape, dt) for i in range(n)], [P.buf(f"{name}{i}") for i in range(n)]
            junk_2, Bjunk_2 = pair("junk", [128, D], BF16, 4)
            ss_2, Bss_2 = pair("ss", [128, 1], F32, 4)
            rstd_2, Brstd_2 = pair("rstd", [128, 1], F32, 3)
            xb_2, Bxb_2 = pair("xb", [128, D], BF16, 4)
            xT_2, BxT_2 = pair("xT", [128, 8, 128], BF16, 3)
            proj_2, Bproj_2 = pair("proj", [128, E], F32, 3)
            sq_2, Bsq_2 = pair("sq", [128, 2048], F32)
            sqn_2, Bsqn_2 = pair("sqn", [128, 2048], F32)
            ssh_2, Bssh_2 = pair("ssh", [128, 32], F32)
            rsh_2, Brsh_2 = pair("rsh", [128, 32], F32)
            pb = [sbuf(st, f"pb{i}", [128, 3072 if layer else E0], BF16) for i in range(2)]
            Bpb = [P.buf(f"pb{i}", dma=True) for i in range(2)]
            BpbQ = [P.buf(f"pbq{i}") for i in range(2)]
            BpbV = [P.buf(f"pbv{i}") for i in range(2)]
            Btp = P.buf("tp")
            Bps = [P.buf(f"psq{i}") for i in range(4)]
            if layer == 1:
                bfr = sbuf(st, "bfr", [128, 16], F32); Bbfr = P.buf("bfr")
                ucum = sbuf(st, "ucum", [128, 128], F32); Buc = P.buf("ucum")
                elast = sbuf(st, "elast", [128, 128], F32); Bel = P.buf("elast")
                zf = sbuf(st, "zf", [128, 16], F32); Bzf = P.buf("zf")
                cprev = sbuf(st, "cprev", [128, 16], F32); Bcp = P.buf("cprev")
                ncs = [sbuf(st, f"ncs{i}", [128, 16], F32) for i in range(2)]
                Bncs = [P.buf(f"ncs{i}", dma=True) for i in range(2)]
                aug = sbuf(st, "aug", [128, 16, 3], BF16); Baug = P.buf("aug")
                r1 = sbuf(st, "r1", [128, 16], F32); Br1 = P.buf("r1")
                augT = [sbuf(st, f"augT{i}", [48, 128], BF16) for i in range(2)]
                BaugT = [P.buf(f"augT{i}", dma=True) for i in range(2)]
                Bpc = P.buf("pc")
                P.dma("sp", bfr[:], dr["bf_r"], chC, writes=[Bbfr], serial=True)
                P.dma("sp", ucum[:], dr["ucum"], chC, writes=[Buc], serial=True)
                P.dma("sp", elast[:], dr["elast"], chC, writes=[Bel], serial=True)
                P.op("dve", "memset", cprev[:], 0.0, writes=[Bcp])
            P.dma("sp", gt[:], dr["g_mix_t"][layer], chC, writes=[Bg], serial=True)
            P.dma("sp", idt[:], dr["ident"], chC, writes=[Bid], serial=True)
            P.dma("sp", gains[:].rearrange("p h d -> p (h d)"), dr["gains0" if layer == 0 else "gains1"], chC,
                  writes=[Bgn], serial=True)
            if layer == 0:
                P.op("dve", "tensor_scalar", out=gains[:, 0:8, :], in0=gains[:, 0:8, :], scalar1=0.125, scalar2=None,
                     op0=ALU.mult, reads=[Bgn], writes=[Bgn])
                P.op("dve", "tensor_scalar", out=gains[:, 16:24, :], in0=gains[:, 16:24, :], scalar1=0.125,
                     scalar2=None, op0=ALU.mult, reads=[Bgn], writes=[Bgn])
            else:
                P.op("dve", "tensor_scalar", out=gains[:, 0:16, :], in0=gains[:, 0:16, :], scalar1=0.125, scalar2=None,
                     op0=ALU.mult, reads=[Bgn], writes=[Bgn])
            load_weight_bf16(st, "win", w_in, D, E, gt, stage, Bst, Wb, BW, Bg)

            tpb = psum[0][:].bitcast(BF16)
            chunks = []
            c0 = 0
            while c0 < E:
                chunks.append((c0, min(512, E - c0)))
                c0 += 512
            Bpsb = {b: P.buf(f"psb{b}") for b in range(1, 8)}
            if layer == 0:
                cgroups = [[(0, 1), (1, 2), (2, 3)], [(3, 4), (4, 5)]]
            else:
                cgroups = [[(0, 1), (1, 2), (2, 3)], [(3, 4), (4, 5), (5, 6), (6, 7)]]
            if layer == 0:
                groups = [(0, 16, 0), (1536, 10, 16)]
                vcopies = [(1024, 512), (2176, 128)]
            else:
                groups = [(0, 32, 0)]
                vcopies = [(2048, 1024)]
            cctr = [0]

            def rb(t):
                s = t % 2
                s3 = t % 3
                s4 = t % 4
                return (s, ss_2[s4], Bss_2[s4], rstd_2[s3], Brstd_2[s3], xb_2[s4], Bxb_2[s4], xT_2[s3], BxT_2[s3],
                        proj_2[s3], Bproj_2[s3])

            def front0(t):
                s, ss, Bss, rstd, Brstd, xb, Bxb, xT, BxT, proj, Bproj = rb(t)
                if t == 0:
                    for tt in (0, 1):
                        if tt < T:
                            P.dma("sp", xt[tt % 2][:], xin[tt * 128:(tt + 1) * 128, :], Bxt[tt % 2], writes=[Bxt[tt % 2]])
                P.op("dve", "tensor_copy", out=xb[:], in_=xt[s][:], reads=[Bxt[s]], writes=[Bxb])
                P.op("act", "activation", out=junk_2[t % 4][:], in_=xt[s][:], func=AF.Square, accum_out=ss[:],
                     reads=[Bxt[s]], writes=[Bss, Bjunk_2[t % 4]])
                if t + 2 < T:
                    P.dma("sp", xt[s][:], xin[(t + 2) * 128:(t + 3) * 128, :], Bxt[s], writes=[Bxt[s]])

            def front1a(t):
                s, ss, Bss, rstd, Brstd, xb, Bxb, xT, BxT, proj, Bproj = rb(t)
                for kc in range(8):
                    P.op("pe", "transpose", out=tpb[:, kc * 128:(kc + 1) * 128], in_=xb[:, kc * 128:(kc + 1) * 128],
                         identity=idt[:], reads=[Bxb, Bid], writes=[Btp])
                P.op("act", "activation", out=xT[:].rearrange("p c t -> p (c t)"), in_=tpb[:, 0:1024], func=AF.Copy,
                     reads=[Btp], writes=[BxT])
                rstd_chain(ss[:], Bss, rstd[:], Brstd, 1, D)

            def front1b(t):
                s, ss, Bss, rstd, Brstd, xb, Bxb, xT, BxT, proj, Bproj = rb(t)
                for grp in cgroups:
                    for kc in range(8):
                        for (ci, bank) in grp:
                            c0, cw = chunks[ci]
                            P.op("pe", "matmul", out=psum[bank][:, 0:cw], lhsT=xT[:, kc, :], rhs=Wb[:, kc, c0:c0 + cw],
                                 start=(kc == 0), stop=(kc == 7), reads=[BxT, BW], writes=[Bpsb[bank]])
                    for (ci, bank) in grp:
                        c0, cw = chunks[ci]
                        P.op("act", "activation", out=proj[:, c0:c0 + cw], in_=psum[bank][:, 0:cw], func=AF.Copy,
                             scale=rstd[:], reads=[Bpsb[bank], Brstd], writes=[Bproj], nowaw=True)

            def group_meta():
                so = 0
                hs = 0
                meta = []
                for (g0, nh, gs) in groups:
                    meta.append((g0, nh, gs, so, hs))
                    so += nh * 64
                    hs += nh
                return meta, hs

            def backA(t):
                    s = t % 2
                    s3 = t % 3
                    proj, Bproj, sq, Bsq, ssh, Bssh, rsh, Brsh = proj_2[s3], Bproj_2[s3], sq_2[s], Bsq_2[s], ssh_2[s], Bssh_2[s], rsh_2[s], Brsh_2[s]
                    meta, hs = group_meta()
                    for (g0, nh, gs, so_, hs_) in meta:
                        w = nh * 64
                        P.op("pool", "tensor_tensor", out=sq[:, so_:so_ + w], in0=proj[:, g0:g0 + w], in1=proj[:, g0:g0 + w],
                             op=ALU.mult, reads=[Bproj], writes=[Bsq], nowaw=True)
                    for (g0, nh, gs, so_, hs_) in meta:
                        sv = sq[:, so_:so_ + nh * 64].rearrange("p (h d) -> p h d", d=64)
                        P.op("dve", "tensor_reduce", out=ssh[:, hs_:hs_ + nh], in_=sv, axis=AX.X, op=ALU.add, reads=[Bsq],
                             writes=[Bssh], nowaw=True)
                    rstd_chain(ssh[:, 0:hs], Bssh, rsh[:, 0:hs], Brsh, hs, 64)

            def backB(t):
                    s = t % 2
                    s3 = t % 3
                    proj, Bproj, sqn, Bsqn, rsh, Brsh = proj_2[s3], Bproj_2[s3], sqn_2[s], Bsqn_2[s], rsh_2[s], Brsh_2[s]
                    meta, hs = group_meta()
                    for (g0, nh, gs, so_, hs_) in meta:
                        w = nh * 64
                        pv = proj[:, g0:g0 + w].rearrange("p (h d) -> p h d", d=64)
                        sv = sqn[:, so_:so_ + w].rearrange("p (h d) -> p h d", d=64)
                        P.op("dve", "tensor_tensor", out=sv, in0=pv,
                             in1=rsh[:, hs_:hs_ + nh].unsqueeze(2).broadcast_to([128, nh, 64]),
                             op=ALU.mult, reads=[Bproj, Brsh], writes=[Bsqn])
                        P.op("pool", "tensor_tensor", out=pb[s][:, g0:g0 + w].rearrange("p (h d) -> p h d", d=64), in0=sv,
                             in1=gains[:, gs:gs + nh, :], op=ALU.mult, reads=[Bsqn, Bgn], writes=[BpbQ[s]], nowaw=True)
                    for (v0, vw) in vcopies:
                        P.op("act", "activation", out=pb[s][:, v0:v0 + vw], in_=proj[:, v0:v0 + vw], func=AF.Copy,
                             reads=[Bproj], writes=[BpbV[s]], nowaw=True)
                    if layer == 0:
                        P.dma("pool", dr["p0"][t * 128:(t + 1) * 128, :], pb[s][:], Bpb[s], reads=[BpbQ[s], BpbV[s]])
                    else:
                        P.dma("pool", dr["p1"][t * 128:(t + 1) * 128, :], pb[s][:], Bpb[s], reads=[BpbQ[s], BpbV[s]])

            def back2(t):
                    s = t % 2
                    s3 = t % 3
                    proj, Bproj = proj_2[s3], Bproj_2[s3]
                    if layer == 1:
                        P.op("dve", "tensor_tensor", out=zf[:], in0=proj[:, 3072:3088], in1=bfr[:], op=ALU.add,
                             reads=[Bproj, Bbfr], writes=[Bzf])
                        P.op("act", "activation", out=zf[:], in_=zf[:], func=AF.Exp, scale=-1.0, reads=[Bzf], writes=[Bzf])
                        P.op("act", "activation", out=zf[:], in_=zf[:], func=AF.Ln, bias=1.0, reads=[Bzf], writes=[Bzf])
                        pc = psum[7][:, 64:80]
                        P.op("pe", "matmul", out=pc, lhsT=ucum[:], rhs=zf[:], start=True, stop=False,
                             reads=[Buc, Bzf], writes=[Bpc])
                        P.op("pe", "matmul", out=pc, lhsT=elast[:], rhs=cprev[:], start=False, stop=True,
                             reads=[Bel, Bcp], writes=[Bpc])
                        P.op("dve", "tensor_copy", out=cprev[:], in_=pc, reads=[Bpc], writes=[Bcp])
                        P.op("dve", "tensor_scalar", out=ncs[s][:], in0=cprev[:], scalar1=-1.0, scalar2=None, op0=ALU.mult,
                             reads=[Bcp], writes=[Bncs[s]])
                        P.dma("pool", dr["negc"][t * 128:(t + 1) * 128, :], ncs[s][:], Bncs[s], reads=[Bncs[s]])
                        P.op("dve", "tensor_copy", out=aug[:, :, 0], in_=cprev[:], reads=[Bcp], writes=[Baug])
                        P.op("dve", "tensor_tensor", out=r1[:], in0=cprev[:], in1=aug[:, :, 0], op=ALU.subtract,
                             reads=[Bcp, Baug], writes=[Br1])
                        P.op("dve", "tensor_copy", out=aug[:, :, 1], in_=r1[:], reads=[Br1], writes=[Baug])
                        P.op("dve", "tensor_tensor", out=r1[:], in0=r1[:], in1=aug[:, :, 1], op=ALU.subtract,
                             reads=[Br1, Baug], writes=[Br1])
                        P.op("dve", "tensor_copy", out=aug[:, :, 2], in_=r1[:], reads=[Br1], writes=[Baug])
                        tq = psum[7][:].bitcast(BF16)[:, 256:384]
                        P.op("pe", "transpose", out=tq[0:48, 0:128], in_=aug[:].rearrange("p h c -> p (h c)"),
                             identity=idt[:], reads=[Baug, Bid], writes=[Bpc6])
                        P.op("act", "activation", out=augT[s][:], in_=tq[0:48, 0:128], func=AF.Copy, reads=[Bpc6],
                             writes=[BaugT[s]])
                        P.dma("pool", dr["caugT"][:, t * 128:(t + 1) * 128], augT[s][:], BaugT[s], reads=[BaugT[s]])

            for tt in range(min(4, T)):
                front0(tt)
            for tt in range(min(2, T)):
                front1a(tt)
                front1b(tt)
            if T > 2:
                front1a(2)
            backA(0)
            for t in range(T):
                if t + 4 < T:
                    front0(t + 4)
                if t + 3 < T:
                    front1a(t + 3)
                if t + 1 < T:
                    backA(t + 1)
                backB(t)
                if t + 2 < T:
                    front1b(t + 2)
                back2(t)
        P.barrier()

    Bpc6 = P.buf("pc6")

    def attn_phase(layer):
        psrc = dr["p0"] if layer == 0 else dr["p1"]
        yT = dr["yT0"] if layer == 0 else dr["yT1"]
        heads = []
        if layer == 0:
            for h in range(8):
                heads.append(dict(kind="A", h=h, q=h * 64, k=512 + h * 64, v=1024 + h * 64))
            for kv in range(2):
                heads.append(dict(kind="B2", h=kv, q=1536 + kv * 256, k=2048 + kv * 64, v=2176 + kv * 64))
        else:
            for h in range(16):
                heads.append(dict(kind="C", h=h, q=h * 64, k=1024 + h * 64, v=2048 + h * 64))
        KR = 67
        with ExitStack() as st:
            chC = P.chan("chC")
            chV = [P.chan(f"chV{i}") for i in range(2)]
            chA = [P.chan(f"chA{i}") for i in range(2)]
            idt = sbuf(st, "idt", [128, 128], BF16); Bid = P.buf("idt")
            P.dma("sp", idt[:], dr["ident"], chC, writes=[Bid], serial=True)
            QT = [sbuf(st, f"QT{i}", [KR, S], BF16) for i in range(2)]
            KT = [sbuf(st, f"KT{i}", [KR, S], BF16) for i in range(2)]
            VV = [sbuf(st, f"VV{i}", [128, T, 128], BF16) for i in range(2)]
            BQ = [[P.buf(f"QT{i}_{c}") for c in range(NG)] for i in range(2)]
            BK = [[P.buf(f"KT{i}_{c}") for c in range(NG)] for i in range(2)]
            BV = [[P.buf(f"VV{i}_{c}") for c in range(NG)] for i in range(2)]
            BVones = P.buf("vones")
            stq = [sbuf(st, f"stq{i}", [128, 4, 128], BF16) for i in range(4)]
            Bstqz = P.buf("stqz")
            for i in range(4):
                P.op("pool", "memset", stq[i][:], 0.0, writes=[Bstqz])
            Bstq = [P.buf(f"stq{i}", dma=True) for i in range(4)]
            PT = [sbuf(st, f"PT{i}", [128, 512], BF16) for i in range(5)]
            BPT = [P.buf(f"PT{i}") for i in range(5)]
            SB = [psum[0], psum[1], psum[2], psum[7], psum[6]]
            den = sbuf(st, "den", [64, 512], F32); Bden = P.buf("den")
            dscr = sbuf(st, "dscr", [64, 512], F32); Bdscr = P.buf("dscr")
            rden = sbuf(st, "rden", [64, 512], F32); Brden = P.buf("rden")

            def fast_recip(use_act):
                if use_act:
                    P.op("act", "activation", out=dscr[:], in_=den[:], func=AF.Ln, reads=[Bden], writes=[Bdscr])
                    P.op("act", "activation", out=rden[:], in_=dscr[:], func=AF.Exp, scale=-1.0, reads=[Bdscr],
                         writes=[Brden])
                else:
                    P.op("dve", "reciprocal", out=rden[:], in_=den[:], reads=[Bden], writes=[Brden])
            ysb = [sbuf(st, f"ysb{i}", [64, 512], BF16) for i in range(2)]
            Bysb = [P.buf(f"ysb{i}", dma=True) for i in range(2)]
            BS = [P.buf(f"S{i}") for i in range(5)]
            BO = [P.buf(f"O{i}") for i in range(2)]
            OB4 = [psum[3], psum[4], psum[7], psum[6]]
            BO4 = [BO[0], BO[1], BS[3], BS[4]]
            _tp = P.buf("tp")
            Btp = [_tp, _tp]
            for i in range(2):
                P.op("pool", "memset", VV[i][:, :, 64:128], 1.0, writes=[BVones])
            if layer == 0:
                amask = sbuf(st, "amask", [128, 20, 512], BF16); Bam = P.buf("amask")
                abias = sbuf(st, "abias", [128, 160], F32); Bab = P.buf("abias")
                bbh = sbuf(st, "bbh", [128, 2048], BF16); Bbh = P.buf("bbh")
                bbl = sbuf(st, "bbl", [128, 2048], BF16); Bbl = P.buf("bbl")
                sk_ = sbuf(st, "sinks", [128, 8], F32); Bsk = P.buf("sinks")
                P.dma("sp", amask[:], dr["amask"], chC, writes=[Bam], serial=True)
                P.dma("sp", abias[:], dr["abias"], chC, writes=[Bab], serial=True)
                P.dma("sp", bbh[:], dr["bb_hi"], chC, writes=[Bbh], serial=True)
                P.dma("sp", bbl[:], dr["bb_lo"], chC, writes=[Bbl], serial=True)
                P.dma("sp", sk_[:], dr["sinks_r"], chC, writes=[Bsk], serial=True)
                P.op("act", "activation", out=sk_[:], in_=sk_[:], func=AF.Exp, reads=[Bsk], writes=[Bsk])
                for i in range(2):
                    P.op("pool", "memset", KT[i][64:67, :], 0.0, writes=[BVones])
                    P.op("pool", "memset", QT[i][64:67, :], 0.0, writes=[BVones])
                sinkrow = sbuf(st, "sinkrow", [64, 2, 512], F32); Bsr = P.buf("sinkrow")
                for h in range(8):
                    P.op("dve", "tensor_copy", out=sinkrow[0:64, h // 4, (h % 4) * 128:(h % 4 + 1) * 128],
                         in_=sk_[0:64, h:h + 1].broadcast_to([64, 128]), reads=[Bsk], writes=[Bsr])
                QTB = [sbuf(st, f"QTB{i}", [67, 4, 512], BF16) for i in range(2)]
                for i in range(2):
                    P.op("pool", "memset", QTB[i][64:67, :, :], 0.0, writes=[BVones])
                BQB = [P.buf(f"QTB{i}") for i in range(2)]
                stqB = [sbuf(st, f"stqB{i}", [128, 4, 256], BF16) for i in range(2)]
                BstqB = [P.buf(f"stqB{i}", dma=True) for i in range(2)]
            else:
                negc = sbuf(st, "negc", [128, T, 16], F32); Bnc = P.buf("negc")
                ndg = sbuf(st, "ndg", [128, 128], BF16); Bnd = P.buf("ndg")
                P.dma("sp", negc[:], dr["negc"].rearrange("(b p) h -> p b h", p=128), chC, writes=[Bnc], serial=True)
                P.dma("sp", ndg[:], dr["negdiag"], chC, writes=[Bnd], serial=True)
                for i in range(2):
                    P.op("pool", "memset", KT[i][64:67, :], 1.0, writes=[BVones])

            stq_ctr = [0]

            def load_dma(hi, c):
                hd = heads[hi]
                s = hi % 2
                for w, col in ((0, hd["q"]), (1, hd["k"])):
                    if w == 0 and hd["kind"] == "B2":
                        continue
                    i = (c % 2) * 2 + w
                    P.dma("sp", stq[i][:, :, 0:64], psrc[c * 512:(c + 1) * 512, col:col + 64].rearrange("(j p) d -> p j d", p=128),
                          Bstq[i], reads=[Bstqz], writes=[Bstq[i]])
                P.dma("sp", VV[s][:, c * 4:(c + 1) * 4, 0:64],
                      psrc[c * 512:(c + 1) * 512, hd["v"]:hd["v"] + 64].rearrange("(j p) d -> p j d", p=128),
                      chV[s], reads=[BVones], writes=[BV[s][c]], serial=True)

            def load_tr(hi, c):
                hd = heads[hi]
                s = hi % 2
                for w, dst, Bd in ((0, QT[s], BQ[s][c]), (1, KT[s], BK[s][c])):
                    if w == 0 and hd["kind"] == "B2":
                        continue
                    i = (c % 2) * 2 + w
                    tb = Btp[0]
                    tpv = psum[5][:].bitcast(BF16)
                    for j in range(4):
                        P.op("pe", "transpose", out=tpv[:, j * 128:(j + 1) * 128], in_=stq[i][:, j, :],
                             identity=idt[:], reads=[Bstq[i], Bid, Bstqz], writes=[tb])
                    P.op("dve", "tensor_copy", out=dst[0:64, c * 512:(c + 1) * 512], in_=tpv[0:64, 0:512],
                         reads=[tb, BVones], writes=[Bd])
                if layer == 1:
                    h = hd["h"]
                    P.dma("sp", QT[s][64:67, c * 512:(c + 1) * 512], dr["caugT"][h * 3:(h + 1) * 3, c * 512:(c + 1) * 512],
                          chA[s], writes=[BQ[s][c]], serial=True)

            def load_chunk(hi, c):
                load_dma(hi, c)
                load_tr(hi, c)

            def units_for(hd, G):
                us = []
                kind = hd["kind"]
                lo = {"A": -16, "B": -1, "C": -4 * G}[kind]
                for rel in range(lo, 4):
                    kb = 4 * G + rel
                    if kb < 0:
                        continue
                    if kind == "A":
                        jlo, jhi = max(0, rel), min(3, rel + 16)
                    elif kind == "B":
                        jlo, jhi = max(0, rel), min(3, rel + 1)
                    else:
                        jlo, jhi = max(0, rel), 3
                    us.append((kb, rel, jlo, jhi))
                return us

            pend = []
            state = {"sctr": 0}

            def emit_S(hi, G, u, first, last):
                hd = heads[hi]
                s = hi % 2
                kb, rel, jlo, jhi = u
                si = state["sctr"] % 5
                state["sctr"] += 1
                c0, c1 = jlo * 128, (jhi + 1) * 128
                kind = hd["kind"]
                extra = (kind != "A")
                rd = [BQ[s][G], BK[s][kb // 4], BVones]
                P.op("pe", "matmul", out=SB[si][:, c0:c1], lhsT=KT[s][0:KR, kb * 128:(kb + 1) * 128],
                     rhs=QT[s][0:KR, G * 512 + c0:G * 512 + c1], start=True,
                     stop=(kind == "A") or (kind == "C" and rel < 0), reads=rd, writes=[BS[si]])
                if kind == "B":
                    h = hd["h"]
                    for j in range(jlo, jhi + 1):
                        d = j - rel
                        o = (h * 2 + d) * 128
                        P.op("pe", "matmul", out=SB[si][:, j * 128:(j + 1) * 128], lhsT=idt[:], rhs=bbh[:, o:o + 128],
                             start=False, stop=False, reads=[Bid, Bbh], writes=[BS[si]])
                        P.op("pe", "matmul", out=SB[si][:, j * 128:(j + 1) * 128], lhsT=idt[:], rhs=bbl[:, o:o + 128],
                             start=False, stop=(j == jhi), reads=[Bid, Bbl], writes=[BS[si]])
                elif kind == "C":
                    if rel >= 0:
                        P.op("pe", "matmul", out=SB[si][:, rel * 128:(rel + 1) * 128], lhsT=idt[:], rhs=ndg[:],
                             start=False, stop=True, reads=[Bid, Bnd], writes=[BS[si]])
                    else:
                        pass
                if kind == "A":
                    bi = hd["h"] * 20 + (rel + 16)
                    P.op("act", "activation", out=PT[si][:, c0:c1], in_=SB[si][:, c0:c1], func=AF.Exp,
                         bias=abias[:, bi:bi + 1], reads=[BS[si], Bab], writes=[BPT[si]])
                    P.op("dve", "tensor_tensor", out=PT[si][:, c0:c1], in0=PT[si][:, c0:c1],
                         in1=amask[:, rel + 16, c0:c1], op=ALU.mult, reads=[BPT[si], Bam], writes=[BPT[si]])
                elif kind == "B":
                    P.op("act", "activation", out=PT[si][:, c0:c1], in_=SB[si][:, c0:c1], func=AF.Exp,
                         reads=[BS[si]], writes=[BPT[si]])
                else:
                    P.op("act", "activation", out=PT[si][:, c0:c1], in_=SB[si][:, c0:c1], func=AF.Exp,
                         bias=negc[:, kb, hd["h"]:hd["h"] + 1], reads=[BS[si], Bnc], writes=[BPT[si]])
                return (hi, G, u, first, last, si)

            octr = {"n": 0}

            def emit_PV(item):
                hi, G, u, first, last, si = item
                hd = heads[hi]
                s = hi % 2
                kb, rel, jlo, jhi = u
                c0, c1 = jlo * 128, (jhi + 1) * 128
                if first:
                    octr["n"] += 1
                oi = octr["n"] % 2
                P.op("pe", "matmul", out=psum[3 + oi][:, c0:c1], lhsT=VV[s][:, kb, :], rhs=PT[si][:, c0:c1],
                     start=first, stop=last, skip_group_check=True, reads=[BV[s][kb // 4], BPT[si], BVones],
                     writes=[BO[oi]])
                if last:
                    yi = octr["n"] % 2

                    def fin(oi=oi, yi=yi, hi=hi, G=G, kind=hd["kind"]):
                        ob = psum[3 + oi]
                        P.op("dve", "tensor_copy", out=den[:], in_=ob[64:128, :], reads=[BO[oi]], writes=[Bden])
                        fast_recip(kind == "A")
                        P.op("dve", "tensor_tensor", out=ysb[yi][:], in0=ob[0:64, :], in1=rden[:], op=ALU.mult,
                             reads=[BO[oi], Brden], writes=[Bysb[yi]])
                        P.dma("pool", yT[hi * 64:(hi + 1) * 64, G * 512:(G + 1) * 512], ysb[yi][:], Bysb[yi],
                              reads=[Bysb[yi]])

                    fin_q.append([3 if hd["kind"] == "A" else 0, fin])

            def loadQ_B(hi, c):
                hd = heads[hi]
                i = c % 2
                P.dma("sp", stqB[i][:], psrc[c * 512:(c + 1) * 512, hd["q"]:hd["q"] + 256].rearrange("(j p) d -> p j d", p=128),
                      BstqB[i], writes=[BstqB[i]])
                for half in range(2):
                    tb = Btp[half]
                    tpv = psum[5][:].bitcast(BF16)
                    for jl in range(2):
                        jj = half * 2 + jl
                        for hp in range(2):
                            o = (jl * 2 + hp) * 128
                            P.op("pe", "transpose", out=tpv[:, o:o + 128], in_=stqB[i][:, jj, hp * 128:(hp + 1) * 128],
                                 identity=idt[:], reads=[BstqB[i], Bid], writes=[tb])
                    for jl in range(2):
                        jj = half * 2 + jl
                        for h4 in range(4):
                            o = (jl * 2 + h4 // 2) * 128
                            r0 = (h4 % 2) * 64
                            P.op("dve", "tensor_copy", out=QTB[i][0:64, jj, h4 * 128:(h4 + 1) * 128],
                                 in_=tpv[r0:r0 + 64, o:o + 128], reads=[tb, BVones], writes=[BQB[i]])

            def emit_S_B(hi, c, jj, d, first, last):
                hd = heads[hi]
                s = hi % 2
                kv = hd["h"]
                j = 4 * c + jj
                kb = j - d
                si = state["sctr"] % 3
                state["sctr"] += 1
                o = (kv * 2 + d) * 512
                P.op("pe", "matmul", out=SB[si][:, 0:512], lhsT=KT[s][0:67, kb * 128:(kb + 1) * 128],
                     rhs=QTB[c % 2][0:67, jj, :], start=True, stop=False, reads=[BQB[c % 2], BK[s][kb // 4], BVones],
                     writes=[BS[si]])
                P.op("pe", "matmul", out=SB[si][:, 0:512], lhsT=idt[:], rhs=bbh[:, o:o + 512], start=False, stop=False,
                     reads=[Bid, Bbh], writes=[BS[si]])
                P.op("pe", "matmul", out=SB[si][:, 0:512], lhsT=idt[:], rhs=bbl[:, o:o + 512], start=False, stop=True,
                     reads=[Bid, Bbl], writes=[BS[si]])
                P.op("act", "activation", out=PT[si][:, 0:512], in_=SB[si][:, 0:512], func=AF.Exp,
                     reads=[BS[si]], writes=[BPT[si]])
                return ("B", hi, j, kb, first, last, si)

            def emit_PV_B(item):
                _, hi, j, kb, first, last, si = item
                hd = heads[hi]
                s = hi % 2
                kv = hd["h"]
                if first:
                    octr["n"] += 1
                oi = octr["n"] % 4
                P.op("pe", "matmul", out=OB4[oi][:, 0:512], lhsT=VV[s][:, kb, :], rhs=PT[si][:, 0:512],
                     start=first, stop=last, skip_group_check=True, reads=[BV[s][kb // 4], BPT[si], BVones],
                     writes=[BO4[oi]])
                if last:
                    yi = octr["n"] % 2

                    def fin(oi=oi, yi=yi, kv=kv, j=j):
                        ob = OB4[oi]
                        P.op("dve", "tensor_copy", out=den[:], in_=ob[64:128, :], reads=[BO4[oi]], writes=[Bden])
                        P.op("dve", "tensor_tensor", out=den[:], in0=den[:], in1=sinkrow[0:64, kv, :], op=ALU.add,
                             reads=[Bden, Bsr], writes=[Bden])
                        fast_recip(True)
                        P.op("dve", "tensor_tensor", out=ysb[yi][:], in0=ob[0:64, :], in1=rden[:], op=ALU.mult,
                             reads=[BO4[oi], Brden], writes=[Bysb[yi]])
                        for h4 in range(4):
                            r0 = (8 + kv * 4 + h4) * 64
                            P.dma("pool", yT[r0:r0 + 64, j * 128:(j + 1) * 128], ysb[yi][:, h4 * 128:(h4 + 1) * 128],
                                  Bysb[yi], reads=[Bysb[yi]])

                    fin_q.append([2, fin])

            DEPTH = 4
            for c in range(NG):
                load_chunk(0, c)
            fin_q = []

            def run_fins(force=False):
                while fin_q and (force or fin_q[0][0] <= 0):
                    fin_q.pop(0)[1]()

            def do_PV(item):
                for f in fin_q:
                    f[0] -= 1
                if item[0] == "B":
                    emit_PV_B(item)
                else:
                    emit_PV(item)
                run_fins()

            for hi in range(len(heads)):
                isB = heads[hi]["kind"] == "B2"
                if isB:
                    if hi > 0 and heads[hi - 1]["kind"] != "B2":
                        while pend:
                            do_PV(pend.pop(0))
                        run_fins(True)
                    loadQ_B(hi, 0)
                for G in range(NG):
                    if isB:
                        if G + 1 < NG:
                            loadQ_B(hi, G + 1)
                        for jj in range(4):
                            ds = [1, 0] if (4 * G + jj) > 0 else [0]
                            for di, d in enumerate(ds):
                                pend.append(emit_S_B(hi, G, jj, d, di == 0, di == len(ds) - 1))
                                if len(pend) > 2:
                                    do_PV(pend.pop(0))
                    else:
                        us = units_for(heads[hi], G)
                        for ui, u in enumerate(us):
                            pend.append(emit_S(hi, G, u, ui == 0, ui == len(us) - 1))
                            if len(pend) > DEPTH:
                                do_PV(pend.pop(0))
                    if hi + 1 < len(heads):
                        if G == 0:
                            load_dma(hi + 1, 0)
                        if G + 1 < NG:
                            load_dma(hi + 1, G + 1)
                        load_tr(hi + 1, G)
            while pend:
                do_PV(pend.pop(0))
            run_fins(True)
        P.barrier()

    def mlp_phase(layer):
        xin = dr["x"] if layer == 0 else dr["x2"]
        xout = dr["x2"] if layer == 0 else dr["out"]
        yT = dr["yT0"] if layer == 0 else dr["yT1"]
        w_out = dr["w_out0"] if layer == 0 else dr["w_out1"]
        with ExitStack() as st:
            Wup = sbuf(st, "Wup", [128, 8, DFF], BF16); BWu = P.buf("Wup")
            Wdn = sbuf(st, "Wdn", [128, 32, D], BF16); BWd = P.buf("Wdn")
            Wo = sbuf(st, "Wo", [128, 8, D], BF16); BWo = P.buf("Wo")
            stage = [sbuf(st, f"wst{i}", [128, 1024], F32) for i in range(2)]
            Bst = [P.buf(f"wst{i}", dma=True) for i in range(2)]
            chC = P.chan("chC")
            gt = sbuf(st, "gt", [128, 8], F32); Bg = P.buf("gt")
            idt = sbuf(st, "idt", [128, 128], BF16); Bid = P.buf("idt")
            P.dma("sp", gt[:], dr["g_mlp_t"][layer], chC, writes=[Bg], serial=True)
            P.dma("sp", idt[:], dr["ident"], chC, writes=[Bid], serial=True)

            def ldw(src, rows, cols, g_ap, dst, Bdst):
                nr = rows // 128
                k = 0
                for r in range(nr):
                    c0 = 0
                    while c0 < cols:
                        cw = min(1024, cols - c0)
                        s = k % 2
                        k += 1
                        P.dma("sp", stage[s][:, 0:cw], src[r * 128:(r + 1) * 128, c0:c0 + cw], Bst[s], writes=[Bst[s]])
                        kw = dict(scale=g_ap[:, r:r + 1]) if g_ap is not None else {}
                        rd = [Bst[s]] + ([Bg] if g_ap is not None else [])
                        P.op("act", "activation", out=dst[:, r, c0:c0 + cw], in_=stage[s][:, 0:cw], func=AF.Copy,
                             reads=rd, writes=[Bdst], **kw)
                        c0 += cw

            ldw(w_out, D, D, None, Wo, BWo)
            ldw(dr["w_up"][layer], D, DFF, gt, Wup, BWu)
            ldw(dr["w_down"][layer], DFF, D, None, Wdn, BWd)

            xt = [sbuf(st, f"xt{i}", [128, D], F32) for i in range(2)]
            Bxt = [P.buf(f"xt{i}", dma=True) for i in range(2)]
            yt = [sbuf(st, f"yt{i}", [128, 8, 128], BF16) for i in range(2)]
            Byt = [P.buf(f"yt{i}", dma=True) for i in range(2)]
            def pair(name, shape, dt):
                return [sbuf(st, f"{name}{i}", shape, dt) for i in range(2)], [P.buf(f"{name}{i}") for i in range(2)]
            x1_2, Bx1_2 = pair("x1", [128, D], F32)
            junk_2, Bjunk_2 = pair("junk", [128, D], BF16)
            ss_2, Bss_2 = pair("ss", [128, 1], F32)
            rstd_2, Brstd_2 = pair("rstd", [128, 1], F32)
            rstd2_2, Brstd2_2 = pair("rstd2", [128, 1], F32)
            xb_2, Bxb_2 = pair("xb", [128, D], BF16)
            xT_2, BxT_2 = pair("xT", [128, 8, 128], BF16)
            rl = [sbuf(st, f"rl{i}", [128, 512], F32) for i in range(2)]
            Brl = [P.buf(f"rl{i}") for i in range(2)]
            aT = sbuf(st, "aT", [128, 32, 128], BF16); BaT = P.buf("aT")
            xo = [sbuf(st, f"xo{i}", [128, D], F32) for i in range(2)]
            Bxo = [P.buf(f"xo{i}", dma=True) for i in range(2)]
            Bpo = [P.buf(f"po{i}") for i in range(2)]
            Btp = P.buf("tp")
            Bpu = [P.buf(f"pu{i}") for i in range(2)]
            Bpd = [P.buf(f"pd{i}") for i in range(2)]
            tpb = psum[2][:].bitcast(BF16)

            def front_a(t):
                s = t % 2
                x1, Bx1 = x1_2[s], Bx1_2[s]
                P.dma("sp", xt[s][:], xin[t * 128:(t + 1) * 128, :], Bxt[s], writes=[Bxt[s]])
                P.dma("sp", yt[s][:], yT[:, t * 128:(t + 1) * 128].rearrange("(c p) t -> p c t", p=128), Byt[s],
                      writes=[Byt[s]])
                for n in range(2):
                    for kc in range(8):
                        P.op("pe", "matmul", out=psum[n][:], lhsT=yt[s][:, kc, :], rhs=Wo[:, kc, n * 512:(n + 1) * 512],
                             start=(kc == 0), stop=(kc == 7), reads=[Byt[s], BWo], writes=[Bpo[n]])
                    P.op("dve", "tensor_tensor", out=x1[:, n * 512:(n + 1) * 512], in0=psum[n][:],
                         in1=xt[s][:, n * 512:(n + 1) * 512], op=ALU.add, reads=[Bpo[n], Bxt[s]], writes=[Bx1], nowaw=True)
                P.op("act", "activation", out=junk_2[s][:], in_=x1[:], func=AF.Square, accum_out=ss_2[s][:], reads=[Bx1],
                     writes=[Bss_2[s], Bjunk_2[s]])
                rstd_chain(ss_2[s][:], Bss_2[s], rstd_2[s][:], Brstd_2[s], 1, D)
                P.op("dve", "tensor_tensor", out=rstd2_2[s][:], in0=rstd_2[s][:], in1=rstd_2[s][:], op=ALU.mult,
                     reads=[Brstd_2[s]], writes=[Brstd2_2[s]])
                P.op("pool", "tensor_copy", out=xb_2[s][:], in_=x1[:], reads=[Bx1], writes=[Bxb_2[s]])

            def front_b(t):
                s = t % 2
                for kc in range(8):
                    P.op("pe", "transpose", out=tpb[:, kc * 128:(kc + 1) * 128], in_=xb_2[s][:, kc * 128:(kc + 1) * 128],
                         identity=idt[:], reads=[Bxb_2[s], Bid], writes=[Btp])
                P.op("act", "activation", out=xT_2[s][:].rearrange("p c t -> p (c t)"), in_=tpb[:, 0:1024], func=AF.Copy,
                     reads=[Btp], writes=[BxT_2[s]])

            def up(t):
                s = t % 2
                xT, BxT = xT_2[s], BxT_2[s]
                for fg in range(8):
                    b = fg % 2
                    for f4 in range(4):
                        fc = fg * 4 + f4
                        for kc in range(8):
                            P.op("pe", "matmul", out=psum[3 + b][:, f4 * 128:(f4 + 1) * 128],
                                 lhsT=Wup[:, kc, fc * 128:(fc + 1) * 128], rhs=xT[:, kc, :], start=(kc == 0 and f4 == 0),
                                 stop=(kc == 7), skip_group_check=True, reads=[BWu, BxT], writes=[Bpu[b]])
                    P.op("act", "activation", out=rl[b][:], in_=psum[3 + b][:], func=AF.Relu, reads=[Bpu[b]],
                         writes=[Brl[b]])
                    P.op("pool", "tensor_tensor", out=aT[:, fg * 4:(fg + 1) * 4, :].rearrange("p f t -> p (f t)"),
                         in0=rl[b][:], in1=rl[b][:], op=ALU.mult, reads=[Brl[b]], writes=[BaT], nowaw=True)

            def down(t):
                s = t % 2
                x1, Bx1 = x1_2[s], Bx1_2[s]
                for n in range(2):
                    for fc in range(32):
                        P.op("pe", "matmul", out=psum[5 + n][:], lhsT=aT[:, fc, :], rhs=Wdn[:, fc, n * 512:(n + 1) * 512],
                             start=(fc == 0), stop=(fc == 31), reads=[BaT, BWd], writes=[Bpd[n]])
                    P.op("dve", "scalar_tensor_tensor", out=xo[s][:, n * 512:(n + 1) * 512], in0=psum[5 + n][:],
                         scalar=rstd2_2[s][:], in1=x1[:, n * 512:(n + 1) * 512], op0=ALU.mult, op1=ALU.add,
                         reads=[Bpd[n], Brstd2_2[s], Bx1], writes=[Bxo[s]], nowaw=True)
                d = P.dma("pool", xout[t * 128:(t + 1) * 128, :], xo[s][:], Bxo[s], reads=[Bxo[s]])
                if layer == 1:
                    out_dmas.append(d)

            front_a(0)
            front_b(0)
            for t in range(T):
                if t + 1 < T:
                    front_a(t + 1)
                up(t)
                if t + 1 < T:
                    front_b(t + 1)
                down(t)
        P.barrier()

    if 1 in phases:
        qkv_phase(0)
    if 2 in phases:
        attn_phase(0)
    if 3 in phases:
        mlp_phase(0)
    if 4 in phases:
        qkv_phase(1)
    if 5 in phases:
        attn_phase(1)
    if 6 in phases:
        mlp_phase(1)
    stats = P.emit(final_waits=out_dmas[-2:] if out_dmas else [o for o in P.all if o.dsem is not None][-4:])
    return nc, stats


def make_in_maps(inputs, S):
    f = lambda a: np.ascontiguousarray(np.asarray(a, dtype=np.float32))
    x = f(inputs["x"])
    B = x.shape[0]
    shared = {}
    shared["g_mix_t"] = np.ascontiguousarray(f(inputs["g_mix"]).reshape(2, 8, 128).transpose(0, 2, 1))
    shared["g_mlp_t"] = np.ascontiguousarray(f(inputs["g_mlp"]).reshape(2, 8, 128).transpose(0, 2, 1))
    shared["w_in0"] = f(inputs["w_in_even"])[0]
    shared["w_out0"] = f(inputs["w_out_even"])[0]
    shared["w_in1"] = f(inputs["w_in_odd"])[0]
    shared["w_out1"] = f(inputs["w_out_odd"])[0]
    shared["w_up"] = f(inputs["w_up"])
    shared["w_down"] = f(inputs["w_down"])
    aq, ak = f(inputs["a_q_gain"])[0], f(inputs["a_k_gain"])[0]
    bq, bk = f(inputs["b_q_gain"])[0], f(inputs["b_k_gain"])[0]
    g0 = np.concatenate([np.tile(aq, 8), np.tile(ak, 8), np.tile(bq, 8), np.tile(bk, 2)])
    shared["gains0"] = np.ascontiguousarray(np.broadcast_to(g0[None, :], (128, g0.size)))
    cq, ck = f(inputs["c_q_gain"])[0], f(inputs["c_k_gain"])[0]
    g1 = np.concatenate([np.tile(cq, 16), np.tile(ck, 16)])
    shared["gains1"] = np.ascontiguousarray(np.broadcast_to(g1[None, :], (128, g1.size)))
    shared["sinks_r"] = np.ascontiguousarray(np.broadcast_to(f(inputs["b_sinks"])[0][None, :], (128, 8)))
    shared["bf_r"] = np.ascontiguousarray(np.broadcast_to(f(inputs["b_forget"])[0][None, :], (128, 16)))
    shared.update(_consts())
    maps = []
    for b in range(B):
        m = dict(shared)
        m["x"] = np.ascontiguousarray(x[b, :S])
        maps.append(m)
    return maps


_CACHE = {}


def kernel(**inputs):
    x = np.asarray(inputs["x"])
    B, S, _ = x.shape
    if S not in _CACHE:
        _CACHE[S] = build(S)[0]
    nc = _CACHE[S]
    maps = make_in_maps(inputs, S)
    res = run_bass_kernel_spmd(nc, maps, core_ids=list(range(B)))
    return np.stack([np.asarray(r["out"], dtype=np.float32) for r in res.results], axis=0)
```

```python
import numpy as np
import ml_dtypes
from contextlib import ExitStack
import concourse.bass as bass
import concourse.mybir as mybir
from concourse.bass_utils import run_bass_kernel_spmd

F32 = mybir.dt.float32
BF16 = mybir.dt.bfloat16
AF = mybir.ActivationFunctionType
ALU = mybir.AluOpType
AX = mybir.AxisListType

D = 1024
DFF = 4096
HD = 64
EPS = 1e-6
E0 = 2304
E1 = 3088
NEG = -30000.0


class Buf:
    __slots__ = ("name", "w", "r", "dsem")

    def __init__(self, name, dsem=None):
        self.name = name
        self.w = None
        self.r = []
        self.dsem = dsem


class CountSem:
    LIMIT = 30000

    def __init__(self, nc, name):
        self.nc = nc
        self.name = name
        self.cur = None
        self.count = 0
        self.n = 0

    def bump(self, inc):
        if self.cur is None or self.count + inc > self.LIMIT:
            self.cur = self.nc.alloc_semaphore(name=f"{self.name}_{self.n}")
            self.n += 1
            self.count = 0
        self.count += inc
        return (self.cur, self.count)


class Op:
    __slots__ = ("eng", "meth", "args", "kwargs", "deps", "need", "ticket", "dsem")


class Prog:
    ENG = ("pe", "act", "dve", "pool", "sp", "q2")
    ENGOBJ = {"pe": "tensor", "act": "scalar", "dve": "vector", "pool": "gpsimd", "sp": "sync"}

    def __init__(self, nc):
        self.nc = nc
        self.all = []
        self.sems = {e: CountSem(nc, "s" + e) for e in ("pe", "act", "dve", "pool")}
        self.dma_pool = []
        self.phase_ops = {e: None for e in ("pe", "act", "dve", "pool", "sp")}
        self.phase_dmas = []
        self.pending = {}
        self.chan_last = {}

    def buf(self, name, dma=False):
        self.nb = getattr(self, "nb", 0) + 1
        return Buf(name, CountSem(self.nc, f"d{name}_{self.nb}") if dma else None)

    def chan(self, name):
        b = self.buf(name, dma=True)
        return b

    def _finish(self, o, reads, writes, extra_deps=(), nowaw=False):
        deps = list(extra_deps)
        for b in reads:
            if b.w is not None:
                deps.append(b.w)
        for b in writes:
            if b.w is not None and not (nowaw and b.w.eng == o.eng and b.w.dsem is None):
                deps.append(b.w)
            deps.extend(b.r)
        for b in reads:
            b.r.append(o)
        for b in writes:
            b.w = o
            b.r = []
        pend = self.pending.pop(o.eng, None)
        if pend:
            deps.extend(pend)
        seen = set()
        out = []
        for d in deps:
            if id(d) in seen or d is o:
                continue
            seen.add(id(d))
            if o.eng == "pe" and d.eng == "pe" and d.dsem is None:
                continue
            d.need = True
            out.append(d)
        o.deps = out
        self.all.append(o)
        self.phase_ops[o.eng] = o
        return o

    def op(self, eng, meth, *args, reads=(), writes=(), nowaw=False, **kwargs):
        o = Op()
        o.eng, o.meth, o.args, o.kwargs = eng, meth, args, kwargs
        o.need, o.ticket, o.dsem = False, None, None
        return self._finish(o, reads, writes, (), nowaw)

    def dma(self, eng, out, in_, sem_buf, reads=(), writes=(), serial=False, **kwargs):
        o = Op()
        kw = dict(out=out, in_=in_)
        kw.update(kwargs)
        o.eng, o.meth, o.args, o.kwargs = eng, "dma_start", (), kw
        o.need, o.ticket, o.dsem = True, None, sem_buf.dsem
        assert o.dsem is not None, sem_buf.name
        self.phase_dmas.append(o)
        extra = ()
        if serial:
            prev = self.chan_last.get(id(sem_buf))
            if prev is not None:
                extra = (prev,)
            self.chan_last[id(sem_buf)] = o
        return self._finish(o, reads, writes, extra)

    def barrier(self):
        deps = [o for o in self.phase_ops.values() if o is not None] + self.phase_dmas
        last = {}
        for d in deps:
            if d.dsem is not None:
                last[id(d.dsem)] = d
            else:
                last[d.eng] = d
        deps = list(last.values())
        for e in ("pe", "act", "dve", "pool", "sp"):
            self.pending[e] = list(deps)
        self.phase_dmas = []

    def emit(self, final_waits=()):
        nc = self.nc
        for o in self.all:
            if o.dsem is not None:
                o.ticket = o.dsem.bump(16)
            elif o.need:
                o.ticket = self.sems[o.eng].bump(1)
        per = {e: [] for e in self.ENGOBJ}
        for o in self.all:
            per[o.eng].append(o)
        stats = {}

        def run(e, ename, ops, extra_waits):
            waited = {}
            nw = 0
            for o in ops:
                for d in o.deps:
                    h, v = d.ticket
                    if waited.get(h.num, 0) < v:
                        e.wait_ge(h, v)
                        waited[h.num] = v
                        nw += 1
                ins = getattr(e, o.meth)(*o.args, **o.kwargs)
                if o.ticket is not None:
                    ins.then_inc(o.ticket[0], 16 if o.dsem is not None else 1)
            for d in extra_waits:
                h, v = d.ticket
                if waited.get(h.num, 0) < v:
                    e.wait_ge(h, v)
                    waited[h.num] = v
            stats[ename] = (len(ops), nw)

        with nc.Block() as block:
            for ename in self.ENGOBJ:
                ops = per[ename]
                extra = final_waits if ename == "sp" else ()
                if not ops and not extra:
                    continue
                deco = getattr(block, self.ENGOBJ[ename])

                def body(e, ename=ename, ops=ops, extra=extra):
                    run(e, ename, ops, extra)

                deco(body)
        return stats


def _bf(a):
    return np.asarray(a, dtype=np.float32).astype(ml_dtypes.bfloat16)


def _consts():
    c = {}
    c["ident"] = _bf(np.eye(128))
    slopes = (2.0 ** (-8.0 * np.arange(1, 17) / 16)).astype(np.float32)
    sl_b, sl_a = slopes[:8], slopes[8:]
    k = np.arange(128)[:, None]
    q = np.arange(128)[None, :]
    am = np.zeros((128, 20, 512), np.float32)
    for r in range(20):
        rel = r - 16
        for j in range(4):
            dist = (j - rel) * 128 + q - k
            cnt = ((dist >= 0) & (dist <= 128)).astype(np.float32)
            cnt += ((dist >= 0) & (dist <= 512) & (dist % 4 == 0))
            cnt += ((dist >= 0) & (dist <= 2048) & (dist % 16 == 0))
            am[:, r, j * 128:(j + 1) * 128] = cnt
    c["amask"] = _bf(am)
    ab = np.zeros((128, 8, 20), np.float32)
    for h in range(8):
        for r in range(20):
            ab[:, h, r] = sl_a[h] * ((r - 16) * 128 + np.arange(128) - 511)
    c["abias"] = ab.reshape(128, 160)
    bh = np.zeros((128, 8, 2, 128), np.float32)
    for h in range(8):
        for d in range(2):
            dist = d * 128 + q - k
            valid = (dist >= 0) & (dist < 128)
            bh[:, h, d, :] = np.where(valid, -sl_b[h] * dist, NEG)
    bh = np.ascontiguousarray(bh.reshape(128, 2, 4, 2, 128).transpose(0, 1, 3, 2, 4))
    hi = _bf(bh)
    lo = _bf(bh - hi.astype(np.float32))
    c["bb_hi"] = hi.reshape(128, 8 * 2 * 128)
    c["bb_lo"] = lo.reshape(128, 8 * 2 * 128)
    c["negdiag"] = _bf(np.where(k > q, NEG, 0.0))
    c["ucum"] = -(k <= q).astype(np.float32)
    e = np.zeros((128, 128), np.float32)
    e[127, :] = 1.0
    c["elast"] = e
    return c


CONST_SPECS = [("ident", [128, 128], BF16), ("amask", [128, 20, 512], BF16), ("abias", [128, 160], F32),
               ("bb_hi", [128, 2048], BF16), ("bb_lo", [128, 2048], BF16), ("negdiag", [128, 128], BF16),
               ("ucum", [128, 128], F32), ("elast", [128, 128], F32)]

PARAM_SPECS = [("g_mix_t", [2, 128, 8]), ("g_mlp_t", [2, 128, 8]), ("w_in0", [D, E0]), ("w_out0", [D, D]),
               ("w_in1", [D, E1]), ("w_out1", [D, D]), ("w_up", [2, D, DFF]), ("w_down", [2, DFF, D]),
               ("gains0", [128, 26 * 64]), ("gains1", [128, 32 * 64]), ("sinks_r", [128, 8]), ("bf_r", [128, 16])]


def build(S, debug=False, phases=(1, 2, 3, 4, 5, 6)):
    T = S // 128
    NG = S // 512
    nc = bass.Bass("TRN2", target_bir_lowering=False)
    P = Prog(nc)
    dr = {}
    dr["x"] = nc.dram_tensor("x", [S, D], F32, kind="ExternalInput").ap()
    for n, shp in PARAM_SPECS:
        dr[n] = nc.dram_tensor(n, shp, F32, kind="ExternalInput").ap()
    for n, shp, dt in CONST_SPECS:
        dr[n] = nc.dram_tensor(n, shp, dt, kind="ExternalInput").ap()
    dr["out"] = nc.dram_tensor("out", [S, D], F32, kind="ExternalOutput").ap()
    sk = "ExternalOutput" if debug else "Internal"
    dr["p0"] = nc.dram_tensor("p0", [S, E0], BF16, kind=sk).ap()
    dr["yT0"] = nc.dram_tensor("yT0", [D, S], BF16, kind=sk).ap()
    dr["x2"] = nc.dram_tensor("x2", [S, D], F32, kind=sk).ap()
    dr["p1"] = nc.dram_tensor("p1", [S, 3072], BF16, kind=sk).ap()
    dr["negc"] = nc.dram_tensor("negc", [S, 16], F32, kind=sk).ap()
    dr["caugT"] = nc.dram_tensor("caugT", [48, S], BF16, kind=sk).ap()
    dr["yT1"] = nc.dram_tensor("yT1", [D, S], BF16, kind=sk).ap()

    psum = [nc.alloc_psum_tensor(f"ps{i}", [128, 512], F32) for i in range(8)]
    epsc = nc.alloc_sbuf_tensor("epsc", [128, 1], F32)
    Beps = P.buf("epsc")
    P.op("dve", "memset", epsc[:], EPS, writes=[Beps])
    out_dmas = []

    uid = [0]

    def sbuf(st, name, shape, dt):
        uid[0] += 1
        return st.enter_context(nc.sbuf_tensor(f"{name}_{uid[0]}", shape, dt))

    def load_weight_bf16(st, name, src, rows, cols, g_ap, stage, Bst, dst, Bdst, Bgx):
        nr = rows // 128
        for r in range(nr):
            s = r % 2
            c0 = 0
            while c0 < cols:
                cw = min(4096, cols - c0)
                P.dma("sp", stage[s][:, 0:cw], src[r * 128:(r + 1) * 128, c0:c0 + cw], Bst[s], writes=[Bst[s]])
                if g_ap is not None:
                    P.op("act", "activation", out=dst[:, r, c0:c0 + cw], in_=stage[s][:, 0:cw], func=AF.Copy,
                         scale=g_ap[:, r:r + 1], reads=[Bst[s], Bgx], writes=[Bdst])
                else:
                    P.op("act", "activation", out=dst[:, r, c0:c0 + cw], in_=stage[s][:, 0:cw], func=AF.Copy,
                         reads=[Bst[s]], writes=[Bdst])
                c0 += cw
                s ^= 1

    def rstd_chain(ss, Bss, rs, Brs, n, width):
        P.op("act", "activation", out=rs, in_=ss, func=AF.Sqrt, scale=1.0 / width, bias=epsc[:, 0:1], reads=[Bss, Beps],
             writes=[Brs])
        P.op("dve", "reciprocal", out=rs, in_=rs, reads=[Brs], writes=[Brs])

    def qkv_phase(layer):
        E = E0 if layer == 0 else E1
        xin = dr["x"] if layer == 0 else dr["x2"]
        w_in = dr["w_in0"] if layer == 0 else dr["w_in1"]
        nqk = 26 if layer == 0 else 32
        with ExitStack() as st:
            Wb = sbuf(st, "Wb", [128, 8, E], BF16); BW = P.buf("Wb")
            stage = [sbuf(st, f"wst{i}", [128, E], F32) for i in range(2)]
            Bst = [P.buf(f"wst{i}", dma=True) for i in range(2)]
            chC = P.chan("chC")
            gt = sbuf(st, "gt", [128, 8], F32); Bg = P.buf("gt")
            gains = sbuf(st, "gains", [128, nqk, 64], F32); Bgn = P.buf("gains")
            idt = sbuf(st, "idt", [128, 128], BF16); Bid = P.buf("idt")
            xt = [sbuf(st, f"xt{i}", [128, D], F32) for i in range(2)]
            Bxt = [P.buf(f"xt{i}", dma=True) for i in range(2)]
            def pair(name, shape, dt, n=2):
                return [sbuf(st, f"{name}{i}", shape, dt) for i in range(n)], [P.buf(f"{name}{i}") for i in range(n)]
            junk_2, Bjunk_2 = pair("junk", [128, D], BF16, 4)
            ss_2, Bss_2 = pair("ss", [128, 1], F32, 4)
            rstd_2, Brstd_2 = pair("rstd", [128, 1], F32, 3)
            xb_2, Bxb_2 = pair("xb", [128, D], BF16, 4)
            xT_2, BxT_2 = pair("xT", [128, 8, 128], BF16, 3)
            proj_2, Bproj_2 = pair("proj", [128, E], F32, 3)
            sq_2, Bsq_2 = pair("sq", [128, 2048], F32)
            sqn_2, Bsqn_2 = pair("sqn", [128, 2048], F32)
            ssh_2, Bssh_2 = pair("ssh", [128, 32], F32)
            rsh_2, Brsh_2 = pair("rsh", [128, 32], F32)
            pb = [sbuf(st, f"pb{i}", [128, 3072 if layer else E0], BF16) for i in range(2)]
            Bpb = [P.buf(f"pb{i}", dma=True) for i in range(2)]
            BpbQ = [P.buf(f"pbq{i}") for i in range(2)]
            BpbV = [P.buf(f"pbv{i}") for i in range(2)]
            Btp = P.buf("tp")
            Bps = [P.buf(f"psq{i}") for i in range(4)]
            if layer == 1:
                bfr = sbuf(st, "bfr", [128, 16], F32); Bbfr = P.buf("bfr")
                ucum = sbuf(st, "ucum", [128, 128], F32); Buc = P.buf("ucum")
                elast = sbuf(st, "elast", [128, 128], F32); Bel = P.buf("elast")
                zf = sbuf(st, "zf", [128, 16], F32); Bzf = P.buf("zf")
                cprev = sbuf(st, "cprev", [128, 16], F32); Bcp = P.buf("cprev")
                ncs = [sbuf(st, f"ncs{i}", [128, 16], F32) for i in range(2)]
                Bncs = [P.buf(f"ncs{i}", dma=True) for i in range(2)]
                aug = sbuf(st, "aug", [128, 16, 3], BF16); Baug = P.buf("aug")
                r1 = sbuf(st, "r1", [128, 16], F32); Br1 = P.buf("r1")
                augT = [sbuf(st, f"augT{i}", [48, 128], BF16) for i in range(2)]
                BaugT = [P.buf(f"augT{i}", dma=True) for i in range(2)]
                Bpc = P.buf("pc")
                P.dma("sp", bfr[:], dr["bf_r"], chC, writes=[Bbfr], serial=True)
                P.dma("sp", ucum[:], dr["ucum"], chC, writes=[Buc], serial=True)
                P.dma("sp", elast[:], dr["elast"], chC, writes=[Bel], serial=True)
                P.op("dve", "memset", cprev[:], 0.0, writes=[Bcp])
            P.dma("sp", gt[:], dr["g_mix_t"][layer], chC, writes=[Bg], serial=True)
            P.dma("sp", idt[:], dr["ident"], chC, writes=[Bid], serial=True)
            P.dma("sp", gains[:].rearrange("p h d -> p (h d)"), dr["gains0" if layer == 0 else "gains1"], chC,
                  writes=[Bgn], serial=True)
            if layer == 0:
                P.op("dve", "tensor_scalar", out=gains[:, 0:8, :], in0=gains[:, 0:8, :], scalar1=0.125, scalar2=None,
                     op0=ALU.mult, reads=[Bgn], writes=[Bgn])
                P.op("dve", "tensor_scalar", out=gains[:, 16:24, :], in0=gains[:, 16:24, :], scalar1=0.125,
                     scalar2=None, op0=ALU.mult, reads=[Bgn], writes=[Bgn])
            else:
                P.op("dve", "tensor_scalar", out=gains[:, 0:16, :], in0=gains[:, 0:16, :], scalar1=0.125, scalar2=None,
                     op0=ALU.mult, reads=[Bgn], writes=[Bgn])
            load_weight_bf16(st, "win", w_in, D, E, gt, stage, Bst, Wb, BW, Bg)

            tpb = psum[0][:].bitcast(BF16)
            chunks = []
            c0 = 0
            while c0 < E:
                chunks.append((c0, min(512, E - c0)))
                c0 += 512
            Bpsb = {b: P.buf(f"psb{b}") for b in range(1, 8)}
            if layer == 0:
                cgroups = [[(0, 1), (1, 2), (2, 3)], [(3, 4), (4, 5)]]
            else:
                cgroups = [[(0, 1), (1, 2), (2, 3)], [(3, 4), (4, 5), (5, 6), (6, 7)]]
            if layer == 0:
                groups = [(0, 16, 0), (1536, 10, 16)]
                vcopies = [(1024, 512), (2176, 128)]
            else:
                groups = [(0, 32, 0)]
                vcopies = [(2048, 1024)]
            cctr = [0]

            def rb(t):
                s = t % 2
                s3 = t % 3
                s4 = t % 4
                return (s, ss_2[s4], Bss_2[s4], rstd_2[s3], Brstd_2[s3], xb_2[s4], Bxb_2[s4], xT_2[s3], BxT_2[s3],
                        proj_2[s3], Bproj_2[s3])

            def front0(t):
                s, ss, Bss, rstd, Brstd, xb, Bxb, xT, BxT, proj, Bproj = rb(t)
                if t == 0:
                    for tt in (0, 1):
                        if tt < T:
                            P.dma("sp", xt[tt % 2][:], xin[tt * 128:(tt + 1) * 128, :], Bxt[tt % 2], writes=[Bxt[tt % 2]])
                P.op("dve", "tensor_copy", out=xb[:], in_=xt[s][:], reads=[Bxt[s]], writes=[Bxb])
                P.op("act", "activation", out=junk_2[t % 4][:], in_=xt[s][:], func=AF.Square, accum_out=ss[:],
                     reads=[Bxt[s]], writes=[Bss, Bjunk_2[t % 4]])
                if t + 2 < T:
                    P.dma("sp", xt[s][:], xin[(t + 2) * 128:(t + 3) * 128, :], Bxt[s], writes=[Bxt[s]])

            def front1a(t):
                s, ss, Bss, rstd, Brstd, xb, Bxb, xT, BxT, proj, Bproj = rb(t)
                for kc in range(8):
                    P.op("pe", "transpose", out=tpb[:, kc * 128:(kc + 1) * 128], in_=xb[:, kc * 128:(kc + 1) * 128],
                         identity=idt[:], reads=[Bxb, Bid], writes=[Btp])
                P.op("act", "activation", out=xT[:].rearrange("p c t -> p (c t)"), in_=tpb[:, 0:1024], func=AF.Copy,
                     reads=[Btp], writes=[BxT])
                rstd_chain(ss[:], Bss, rstd[:], Brstd, 1, D)

            def front1b(t):
                s, ss, Bss, rstd, Brstd, xb, Bxb, xT, BxT, proj, Bproj = rb(t)
                for grp in cgroups:
                    for kc in range(8):
                        for (ci, bank) in grp:
                            c0, cw = chunks[ci]
                            P.op("pe", "matmul", out=psum[bank][:, 0:cw], lhsT=xT[:, kc, :], rhs=Wb[:, kc, c0:c0 + cw],
                                 start=(kc == 0), stop=(kc == 7), reads=[BxT, BW], writes=[Bpsb[bank]])
                    for (ci, bank) in grp:
                        c0, cw = chunks[ci]
                        P.op("act", "activation", out=proj[:, c0:c0 + cw], in_=psum[bank][:, 0:cw], func=AF.Copy,
                             scale=rstd[:], reads=[Bpsb[bank], Brstd], writes=[Bproj], nowaw=True)

            def group_meta():
                so = 0
                hs = 0
                meta = []
                for (g0, nh, gs) in groups:
                    meta.append((g0, nh, gs, so, hs))
                    so += nh * 64
                    hs += nh
                return meta, hs

            def backA(t):
                    s = t % 2
                    s3 = t % 3
                    proj, Bproj, sq, Bsq, ssh, Bssh, rsh, Brsh = proj_2[s3], Bproj_2[s3], sq_2[s], Bsq_2[s], ssh_2[s], Bssh_2[s], rsh_2[s], Brsh_2[s]
                    meta, hs = group_meta()
                    for (g0, nh, gs, so_, hs_) in meta:
                        w = nh * 64
                        P.op("dve", "tensor_tensor", out=sq[:, so_:so_ + w], in0=proj[:, g0:g0 + w], in1=proj[:, g0:g0 + w],
                             op=ALU.mult, reads=[Bproj], writes=[Bsq], nowaw=True)
                    for (g0, nh, gs, so_, hs_) in meta:
                        sv = sq[:, so_:so_ + nh * 64].rearrange("p (h d) -> p h d", d=64)
                        P.op("dve", "tensor_reduce", out=ssh[:, hs_:hs_ + nh], in_=sv, axis=AX.X, op=ALU.add, reads=[Bsq],
                             writes=[Bssh], nowaw=True)
                    rstd_chain(ssh[:, 0:hs], Bssh, rsh[:, 0:hs], Brsh, hs, 64)

            def backB(t):
                    s = t % 2
                    s3 = t % 3
                    proj, Bproj, sqn, Bsqn, rsh, Brsh = proj_2[s3], Bproj_2[s3], sqn_2[s], Bsqn_2[s], rsh_2[s], Brsh_2[s]
                    meta, hs = group_meta()
                    for (g0, nh, gs, so_, hs_) in meta:
                        w = nh * 64
                        pv = proj[:, g0:g0 + w].rearrange("p (h d) -> p h d", d=64)
                        sv = sqn[:, so_:so_ + w].rearrange("p (h d) -> p h d", d=64)
                        P.op("dve", "tensor_tensor", out=sv, in0=pv,
                             in1=rsh[:, hs_:hs_ + nh].unsqueeze(2).broadcast_to([128, nh, 64]),
                             op=ALU.mult, reads=[Bproj, Brsh], writes=[Bsqn])
                        P.op("pool", "tensor_tensor", out=pb[s][:, g0:g0 + w].rearrange("p (h d) -> p h d", d=64), in0=sv,
                             in1=gains[:, gs:gs + nh, :], op=ALU.mult, reads=[Bsqn, Bgn], writes=[BpbQ[s]], nowaw=True)
                    for (v0, vw) in vcopies:
                        P.op("act", "activation", out=pb[s][:, v0:v0 + vw], in_=proj[:, v0:v0 + vw], func=AF.Copy,
                             reads=[Bproj], writes=[BpbV[s]], nowaw=True)
                    if layer == 0:
                        P.dma("pool", dr["p0"][t * 128:(t + 1) * 128, :], pb[s][:], Bpb[s], reads=[BpbQ[s], BpbV[s]])
                    else:
                        P.dma("pool", dr["p1"][t * 128:(t + 1) * 128, :], pb[s][:], Bpb[s], reads=[BpbQ[s], BpbV[s]])

            def back2(t):
                    s = t % 2
                    s3 = t % 3
                    proj, Bproj = proj_2[s3], Bproj_2[s3]
                    if layer == 1:
                        P.op("dve", "tensor_tensor", out=zf[:], in0=proj[:, 3072:3088], in1=bfr[:], op=ALU.add,
                             reads=[Bproj, Bbfr], writes=[Bzf])
                        P.op("act", "activation", out=zf[:], in_=zf[:], func=AF.Exp, scale=-1.0, reads=[Bzf], writes=[Bzf])
                        P.op("act", "activation", out=zf[:], in_=zf[:], func=AF.Ln, bias=1.0, reads=[Bzf], writes=[Bzf])
                        pc = psum[7][:, 64:80]
                        P.op("pe", "matmul", out=pc, lhsT=ucum[:], rhs=zf[:], start=True, stop=False,
                             reads=[Buc, Bzf], writes=[Bpc])
                        P.op("pe", "matmul", out=pc, lhsT=elast[:], rhs=cprev[:], start=False, stop=True,
                             reads=[Bel, Bcp], writes=[Bpc])
                        P.op("dve", "tensor_copy", out=cprev[:], in_=pc, reads=[Bpc], writes=[Bcp])
                        P.op("dve", "tensor_scalar", out=ncs[s][:], in0=cprev[:], scalar1=-1.0, scalar2=None, op0=ALU.mult,
                             reads=[Bcp], writes=[Bncs[s]])
                        P.dma("pool", dr["negc"][t * 128:(t + 1) * 128, :], ncs[s][:], Bncs[s], reads=[Bncs[s]])
                        P.op("dve", "tensor_copy", out=aug[:, :, 0], in_=cprev[:], reads=[Bcp], writes=[Baug])
                        P.op("dve", "tensor_tensor", out=r1[:], in0=cprev[:], in1=aug[:, :, 0], op=ALU.subtract,
                             reads=[Bcp, Baug], writes=[Br1])
                        P.op("dve", "tensor_copy", out=aug[:, :, 1], in_=r1[:], reads=[Br1], writes=[Baug])
                        P.op("dve", "tensor_tensor", out=r1[:], in0=r1[:], in1=aug[:, :, 1], op=ALU.subtract,
                             reads=[Br1, Baug], writes=[Br1])
                        P.op("dve", "tensor_copy", out=aug[:, :, 2], in_=r1[:], reads=[Br1], writes=[Baug])
                        tq = psum[7][:].bitcast(BF16)[:, 256:384]
                        P.op("pe", "transpose", out=tq[0:48, 0:128], in_=aug[:].rearrange("p h c -> p (h c)"),
                             identity=idt[:], reads=[Baug, Bid], writes=[Bpc6])
                        P.op("act", "activation", out=augT[s][:], in_=tq[0:48, 0:128], func=AF.Copy, reads=[Bpc6],
                             writes=[BaugT[s]])
                        P.dma("pool", dr["caugT"][:, t * 128:(t + 1) * 128], augT[s][:], BaugT[s], reads=[BaugT[s]])

            for tt in range(min(4, T)):
                front0(tt)
            for tt in range(min(2, T)):
                front1a(tt)
                front1b(tt)
            if T > 2:
                front1a(2)
            backA(0)
            for t in range(T):
                if t + 4 < T:
                    front0(t + 4)
                if t + 3 < T:
                    front1a(t + 3)
                if t + 1 < T:
                    backA(t + 1)
                backB(t)
                if t + 2 < T:
                    front1b(t + 2)
                back2(t)
        P.barrier()

    Bpc6 = P.buf("pc6")

    def attn_phase(layer):
        psrc = dr["p0"] if layer == 0 else dr["p1"]
        yT = dr["yT0"] if layer == 0 else dr["yT1"]
        heads = []
        if layer == 0:
            for h in range(8):
                heads.append(dict(kind="A", h=h, q=h * 64, k=512 + h * 64, v=1024 + h * 64))
            for kv in range(2):
                heads.append(dict(kind="B2", h=kv, q=1536 + kv * 256, k=2048 + kv * 64, v=2176 + kv * 64))
        else:
            for h in range(16):
                heads.append(dict(kind="C", h=h, q=h * 64, k=1024 + h * 64, v=2048 + h * 64))
        KR = 67
        with ExitStack() as st:
            chC = P.chan("chC")
            chV = [P.chan(f"chV{i}") for i in range(2)]
            chA = [P.chan(f"chA{i}") for i in range(2)]
            idt = sbuf(st, "idt", [128, 128], BF16); Bid = P.buf("idt")
            P.dma("sp", idt[:], dr["ident"], chC, writes=[Bid], serial=True)
            QT = [sbuf(st, f"QT{i}", [KR, S], BF16) for i in range(2)]
            KT = [sbuf(st, f"KT{i}", [KR, S], BF16) for i in range(2)]
            VV = [sbuf(st, f"VV{i}", [128, T, 128], BF16) for i in range(2)]
            BQ = [[P.buf(f"QT{i}_{c}") for c in range(NG)] for i in range(2)]
            BK = [[P.buf(f"KT{i}_{c}") for c in range(NG)] for i in range(2)]
            BV = [[P.buf(f"VV{i}_{c}") for c in range(NG)] for i in range(2)]
            BVones = P.buf("vones")
            stq = [sbuf(st, f"stq{i}", [128, 4, 128], BF16) for i in range(4)]
            Bstqz = P.buf("stqz")
            for i in range(4):
                P.op("pool", "memset", stq[i][:], 0.0, writes=[Bstqz])
            Bstq = [P.buf(f"stq{i}", dma=True) for i in range(4)]
            PT = [sbuf(st, f"PT{i}", [128, 512], BF16) for i in range(5)]
            BPT = [P.buf(f"PT{i}") for i in range(5)]
            SB = [psum[0], psum[1], psum[2], psum[7], psum[6]]
            den = sbuf(st, "den", [64, 512], F32); Bden = P.buf("den")
            dscr = sbuf(st, "dscr", [64, 512], F32); Bdscr = P.buf("dscr")
            rden = sbuf(st, "rden", [64, 512], F32); Brden = P.buf("rden")

            def fast_recip(use_act):
                if use_act:
                    P.op("act", "activation", out=dscr[:], in_=den[:], func=AF.Ln, reads=[Bden], writes=[Bdscr])
                    P.op("act", "activation", out=rden[:], in_=dscr[:], func=AF.Exp, scale=-1.0, reads=[Bdscr],
                         writes=[Brden])
                else:
                    P.op("dve", "reciprocal", out=rden[:], in_=den[:], reads=[Bden], writes=[Brden])
            ysb = [sbuf(st, f"ysb{i}", [64, 512], BF16) for i in range(2)]
            Bysb = [P.buf(f"ysb{i}", dma=True) for i in range(2)]
            BS = [P.buf(f"S{i}") for i in range(5)]
            BO = [P.buf(f"O{i}") for i in range(2)]
            OB4 = [psum[3], psum[4], psum[7], psum[6]]
            BO4 = [BO[0], BO[1], BS[3], BS[4]]
            _tp = P.buf("tp")
            Btp = [_tp, _tp]
            for i in range(2):
                P.op("pool", "memset", VV[i][:, :, 64:128], 1.0, writes=[BVones])
            if layer == 0:
                amask = sbuf(st, "amask", [128, 20, 512], BF16); Bam = P.buf("amask")
                abias = sbuf(st, "abias", [128, 160], F32); Bab = P.buf("abias")
                bbh = sbuf(st, "bbh", [128, 2048], BF16); Bbh = P.buf("bbh")
                bbl = sbuf(st, "bbl", [128, 2048], BF16); Bbl = P.buf("bbl")
                sk_ = sbuf(st, "sinks", [128, 8], F32); Bsk = P.buf("sinks")
                P.dma("sp", amask[:], dr["amask"], chC, writes=[Bam], serial=True)
                P.dma("sp", abias[:], dr["abias"], chC, writes=[Bab], serial=True)
                P.dma("sp", bbh[:], dr["bb_hi"], chC, writes=[Bbh], serial=True)
                P.dma("sp", bbl[:], dr["bb_lo"], chC, writes=[Bbl], serial=True)
                P.dma("sp", sk_[:], dr["sinks_r"], chC, writes=[Bsk], serial=True)
                P.op("act", "activation", out=sk_[:], in_=sk_[:], func=AF.Exp, reads=[Bsk], writes=[Bsk])
                for i in range(2):
                    P.op("pool", "memset", KT[i][64:67, :], 0.0, writes=[BVones])
                    P.op("pool", "memset", QT[i][64:67, :], 0.0, writes=[BVones])
                sinkrow = sbuf(st, "sinkrow", [64, 2, 512], F32); Bsr = P.buf("sinkrow")
                for h in range(8):
                    P.op("dve", "tensor_copy", out=sinkrow[0:64, h // 4, (h % 4) * 128:(h % 4 + 1) * 128],
                         in_=sk_[0:64, h:h + 1].broadcast_to([64, 128]), reads=[Bsk], writes=[Bsr])
                QTB = [sbuf(st, f"QTB{i}", [67, 4, 512], BF16) for i in range(2)]
                for i in range(2):
                    P.op("pool", "memset", QTB[i][64:67, :, :], 0.0, writes=[BVones])
                BQB = [P.buf(f"QTB{i}") for i in range(2)]
                stqB = [sbuf(st, f"stqB{i}", [128, 4, 256], BF16) for i in range(2)]
                BstqB = [P.buf(f"stqB{i}", dma=True) for i in range(2)]
            else:
                negc = sbuf(st, "negc", [128, T, 16], F32); Bnc = P.buf("negc")
                ndg = sbuf(st, "ndg", [128, 128], BF16); Bnd = P.buf("ndg")
                P.dma("sp", negc[:], dr["negc"].rearrange("(b p) h -> p b h", p=128), chC, writes=[Bnc], serial=True)
                P.dma("sp", ndg[:], dr["negdiag"], chC, writes=[Bnd], serial=True)
                for i in range(2):
                    P.op("pool", "memset", KT[i][64:67, :], 1.0, writes=[BVones])

            stq_ctr = [0]

            def load_dma(hi, c):
                hd = heads[hi]
                s = hi % 2
                for w, col in ((0, hd["q"]), (1, hd["k"])):
                    if w == 0 and hd["kind"] == "B2":
                        continue
                    i = (c % 2) * 2 + w
                    P.dma("sp", stq[i][:, :, 0:64], psrc[c * 512:(c + 1) * 512, col:col + 64].rearrange("(j p) d -> p j d", p=128),
                          Bstq[i], reads=[Bstqz], writes=[Bstq[i]])
                P.dma("sp", VV[s][:, c * 4:(c + 1) * 4, 0:64],
                      psrc[c * 512:(c + 1) * 512, hd["v"]:hd["v"] + 64].rearrange("(j p) d -> p j d", p=128),
                      chV[s], reads=[BVones], writes=[BV[s][c]], serial=True)

            def load_tr(hi, c):
                hd = heads[hi]
                s = hi % 2
                for w, dst, Bd in ((0, QT[s], BQ[s][c]), (1, KT[s], BK[s][c])):
                    if w == 0 and hd["kind"] == "B2":
                        continue
                    i = (c % 2) * 2 + w
                    tb = Btp[0]
                    tpv = psum[5][:].bitcast(BF16)
                    for j in range(4):
                        P.op("pe", "transpose", out=tpv[:, j * 128:(j + 1) * 128], in_=stq[i][:, j, :],
                             identity=idt[:], reads=[Bstq[i], Bid, Bstqz], writes=[tb])
                    P.op("dve", "tensor_copy", out=dst[0:64, c * 512:(c + 1) * 512], in_=tpv[0:64, 0:512],
                         reads=[tb, BVones], writes=[Bd])
                if layer == 1:
                    h = hd["h"]
                    P.dma("sp", QT[s][64:67, c * 512:(c + 1) * 512], dr["caugT"][h * 3:(h + 1) * 3, c * 512:(c + 1) * 512],
                          chA[s], writes=[BQ[s][c]], serial=True)

            def load_chunk(hi, c):
                load_dma(hi, c)
                load_tr(hi, c)

            def units_for(hd, G):
                us = []
                kind = hd["kind"]
                lo = {"A": -16, "B": -1, "C": -4 * G}[kind]
                for rel in range(lo, 4):
                    kb = 4 * G + rel
                    if kb < 0:
                        continue
                    if kind == "A":
                        jlo, jhi = max(0, rel), min(3, rel + 16)
                    elif kind == "B":
                        jlo, jhi = max(0, rel), min(3, rel + 1)
                    else:
                        jlo, jhi = max(0, rel), 3
                    us.append((kb, rel, jlo, jhi))
                return us

            pend = []
            state = {"sctr": 0}

            def emit_S(hi, G, u, first, last):
                hd = heads[hi]
                s = hi % 2
                kb, rel, jlo, jhi = u
                si = state["sctr"] % 5
                state["sctr"] += 1
                c0, c1 = jlo * 128, (jhi + 1) * 128
                kind = hd["kind"]
                extra = (kind != "A")
                rd = [BQ[s][G], BK[s][kb // 4], BVones]
                P.op("pe", "matmul", out=SB[si][:, c0:c1], lhsT=KT[s][0:KR, kb * 128:(kb + 1) * 128],
                     rhs=QT[s][0:KR, G * 512 + c0:G * 512 + c1], start=True,
                     stop=(kind == "A") or (kind == "C" and rel < 0), reads=rd, writes=[BS[si]])
                if kind == "B":
                    h = hd["h"]
                    for j in range(jlo, jhi + 1):
                        d = j - rel
                        o = (h * 2 + d) * 128
                        P.op("pe", "matmul", out=SB[si][:, j * 128:(j + 1) * 128], lhsT=idt[:], rhs=bbh[:, o:o + 128],
                             start=False, stop=False, reads=[Bid, Bbh], writes=[BS[si]])
                        P.op("pe", "matmul", out=SB[si][:, j * 128:(j + 1) * 128], lhsT=idt[:], rhs=bbl[:, o:o + 128],
                             start=False, stop=(j == jhi), reads=[Bid, Bbl], writes=[BS[si]])
                elif kind == "C":
                    if rel >= 0:
                        P.op("pe", "matmul", out=SB[si][:, rel * 128:(rel + 1) * 128], lhsT=idt[:], rhs=ndg[:],
                             start=False, stop=True, reads=[Bid, Bnd], writes=[BS[si]])
                    else:
                        pass
                if kind == "A":
                    bi = hd["h"] * 20 + (rel + 16)
                    P.op("act", "activation", out=PT[si][:, c0:c1], in_=SB[si][:, c0:c1], func=AF.Exp,
                         bias=abias[:, bi:bi + 1], reads=[BS[si], Bab], writes=[BPT[si]])
                    P.op("dve", "tensor_tensor", out=PT[si][:, c0:c1], in0=PT[si][:, c0:c1],
                         in1=amask[:, rel + 16, c0:c1], op=ALU.mult, reads=[BPT[si], Bam], writes=[BPT[si]])
                elif kind == "B":
                    P.op("act", "activation", out=PT[si][:, c0:c1], in_=SB[si][:, c0:c1], func=AF.Exp,
                         reads=[BS[si]], writes=[BPT[si]])
                else:
                    P.op("act", "activation", out=PT[si][:, c0:c1], in_=SB[si][:, c0:c1], func=AF.Exp,
                         bias=negc[:, kb, hd["h"]:hd["h"] + 1], reads=[BS[si], Bnc], writes=[BPT[si]])
                return (hi, G, u, first, last, si)

            octr = {"n": 0}

            def emit_PV(item):
                hi, G, u, first, last, si = item
                hd = heads[hi]
                s = hi % 2
                kb, rel, jlo, jhi = u
                c0, c1 = jlo * 128, (jhi + 1) * 128
                if first:
                    octr["n"] += 1
                oi = octr["n"] % 2
                P.op("pe", "matmul", out=psum[3 + oi][:, c0:c1], lhsT=VV[s][:, kb, :], rhs=PT[si][:, c0:c1],
                     start=first, stop=last, skip_group_check=True, reads=[BV[s][kb // 4], BPT[si], BVones],
                     writes=[BO[oi]])
                if last:
                    yi = octr["n"] % 2

                    def fin(oi=oi, yi=yi, hi=hi, G=G, kind=hd["kind"]):
                        ob = psum[3 + oi]
                        P.op("dve", "tensor_copy", out=den[:], in_=ob[64:128, :], reads=[BO[oi]], writes=[Bden])
                        fast_recip(kind == "A")
                        P.op("dve", "tensor_tensor", out=ysb[yi][:], in0=ob[0:64, :], in1=rden[:], op=ALU.mult,
                             reads=[BO[oi], Brden], writes=[Bysb[yi]])
                        P.dma("pool", yT[hi * 64:(hi + 1) * 64, G * 512:(G + 1) * 512], ysb[yi][:], Bysb[yi],
                              reads=[Bysb[yi]])

                    fin_q.append([3 if hd["kind"] == "A" else 0, fin])

            def loadQ_B(hi, c):
                hd = heads[hi]
                i = c % 2
                P.dma("sp", stqB[i][:], psrc[c * 512:(c + 1) * 512, hd["q"]:hd["q"] + 256].rearrange("(j p) d -> p j d", p=128),
                      BstqB[i], writes=[BstqB[i]])
                for half in range(2):
                    tb = Btp[half]
                    tpv = psum[5][:].bitcast(BF16)
                    for jl in range(2):
                        jj = half * 2 + jl
                        for hp in range(2):
                            o = (jl * 2 + hp) * 128
                            P.op("pe", "transpose", out=tpv[:, o:o + 128], in_=stqB[i][:, jj, hp * 128:(hp + 1) * 128],
                                 identity=idt[:], reads=[BstqB[i], Bid], writes=[tb])
                    for jl in range(2):
                        jj = half * 2 + jl
                        for h4 in range(4):
                            o = (jl * 2 + h4 // 2) * 128
                            r0 = (h4 % 2) * 64
                            P.op("dve", "tensor_copy", out=QTB[i][0:64, jj, h4 * 128:(h4 + 1) * 128],
                                 in_=tpv[r0:r0 + 64, o:o + 128], reads=[tb, BVones], writes=[BQB[i]])

            def emit_S_B(hi, c, jj, d, first, last):
                hd = heads[hi]
                s = hi % 2
                kv = hd["h"]
                j = 4 * c + jj
                kb = j - d
                si = state["sctr"] % 3
                state["sctr"] += 1
                o = (kv * 2 + d) * 512
                P.op("pe", "matmul", out=SB[si][:, 0:512], lhsT=KT[s][0:67, kb * 128:(kb + 1) * 128],
                     rhs=QTB[c % 2][0:67, jj, :], start=True, stop=False, reads=[BQB[c % 2], BK[s][kb // 4], BVones],
                     writes=[BS[si]])
                P.op("pe", "matmul", out=SB[si][:, 0:512], lhsT=idt[:], rhs=bbh[:, o:o + 512], start=False, stop=False,
                     reads=[Bid, Bbh], writes=[BS[si]])
                P.op("pe", "matmul", out=SB[si][:, 0:512], lhsT=idt[:], rhs=bbl[:, o:o + 512], start=False, stop=True,
                     reads=[Bid, Bbl], writes=[BS[si]])
                P.op("act", "activation", out=PT[si][:, 0:512], in_=SB[si][:, 0:512], func=AF.Exp,
                     reads=[BS[si]], writes=[BPT[si]])
                return ("B", hi, j, kb, first, last, si)

            def emit_PV_B(item):
                _, hi, j, kb, first, last, si = item
                hd = heads[hi]
                s = hi % 2
                kv = hd["h"]
                if first:
                    octr["n"] += 1
                oi = octr["n"] % 4
                P.op("pe", "matmul", out=OB4[oi][:, 0:512], lhsT=VV[s][:, kb, :], rhs=PT[si][:, 0:512],
                     start=first, stop=last, skip_group_check=True, reads=[BV[s][kb // 4], BPT[si], BVones],
                     writes=[BO4[oi]])
                if last:
                    yi = octr["n"] % 2

                    def fin(oi=oi, yi=yi, kv=kv, j=j):
                        ob = OB4[oi]
                        P.op("dve", "tensor_copy", out=den[:], in_=ob[64:128, :], reads=[BO4[oi]], writes=[Bden])
                        P.op("dve", "tensor_tensor", out=den[:], in0=den[:], in1=sinkrow[0:64, kv, :], op=ALU.add,
                             reads=[Bden, Bsr], writes=[Bden])
                        fast_recip(True)
                        P.op("dve", "tensor_tensor", out=ysb[yi][:], in0=ob[0:64, :], in1=rden[:], op=ALU.mult,
                             reads=[BO4[oi], Brden], writes=[Bysb[yi]])
                        for h4 in range(4):
                            r0 = (8 + kv * 4 + h4) * 64
                            P.dma("pool", yT[r0:r0 + 64, j * 128:(j + 1) * 128], ysb[yi][:, h4 * 128:(h4 + 1) * 128],
                                  Bysb[yi], reads=[Bysb[yi]])

                    fin_q.append([2, fin])

            DEPTH = 4
            for c in range(NG):
                load_chunk(0, c)
            fin_q = []

            def run_fins(force=False):
                while fin_q and (force or fin_q[0][0] <= 0):
                    fin_q.pop(0)[1]()

            def do_PV(item):
                for f in fin_q:
                    f[0] -= 1
                if item[0] == "B":
                    emit_PV_B(item)
                else:
                    emit_PV(item)
                run_fins()

            for hi in range(len(heads)):
                isB = heads[hi]["kind"] == "B2"
                if isB:
                    if hi > 0 and heads[hi - 1]["kind"] != "B2":
                        while pend:
                            do_PV(pend.pop(0))
                        run_fins(True)
                    loadQ_B(hi, 0)
                for G in range(NG):
                    if isB:
                        if G + 1 < NG:
                            loadQ_B(hi, G + 1)
                        for jj in range(4):
                            ds = [1, 0] if (4 * G + jj) > 0 else [0]
                            for di, d in enumerate(ds):
                                pend.append(emit_S_B(hi, G, jj, d, di == 0, di == len(ds) - 1))
                                if len(pend) > 2:
                                    do_PV(pend.pop(0))
                    else:
                        us = units_for(heads[hi], G)
                        for ui, u in enumerate(us):
                            pend.append(emit_S(hi, G, u, ui == 0, ui == len(us) - 1))
                            if len(pend) > DEPTH:
                                do_PV(pend.pop(0))
                    if hi + 1 < len(heads):
                        if G == 0:
                            load_dma(hi + 1, 0)
                        if G + 1 < NG:
                            load_dma(hi + 1, G + 1)
                        load_tr(hi + 1, G)
            while pend:
                do_PV(pend.pop(0))
            run_fins(True)
        P.barrier()

    def mlp_phase(layer):
        xin = dr["x"] if layer == 0 else dr["x2"]
        xout = dr["x2"] if layer == 0 else dr["out"]
        yT = dr["yT0"] if layer == 0 else dr["yT1"]
        w_out = dr["w_out0"] if layer == 0 else dr["w_out1"]
        with ExitStack() as st:
            Wup = sbuf(st, "Wup", [128, 8, DFF], BF16); BWu = P.buf("Wup")
            Wdn = sbuf(st, "Wdn", [128, 32, D], BF16); BWd = P.buf("Wdn")
            Wo = sbuf(st, "Wo", [128, 8, D], BF16); BWo = P.buf("Wo")
            stage = [sbuf(st, f"wst{i}", [128, 1024], F32) for i in range(2)]
            Bst = [P.buf(f"wst{i}", dma=True) for i in range(2)]
            chC = P.chan("chC")
            gt = sbuf(st, "gt", [128, 8], F32); Bg = P.buf("gt")
            idt = sbuf(st, "idt", [128, 128], BF16); Bid = P.buf("idt")
            P.dma("sp", gt[:], dr["g_mlp_t"][layer], chC, writes=[Bg], serial=True)
            P.dma("sp", idt[:], dr["ident"], chC, writes=[Bid], serial=True)

            def ldw(src, rows, cols, g_ap, dst, Bdst):
                nr = rows // 128
                k = 0
                for r in range(nr):
                    c0 = 0
                    while c0 < cols:
                        cw = min(1024, cols - c0)
                        s = k % 2
                        k += 1
                        P.dma("sp", stage[s][:, 0:cw], src[r * 128:(r + 1) * 128, c0:c0 + cw], Bst[s], writes=[Bst[s]])
                        kw = dict(scale=g_ap[:, r:r + 1]) if g_ap is not None else {}
                        rd = [Bst[s]] + ([Bg] if g_ap is not None else [])
                        P.op("act", "activation", out=dst[:, r, c0:c0 + cw], in_=stage[s][:, 0:cw], func=AF.Copy,
                             reads=rd, writes=[Bdst], **kw)
                        c0 += cw

            ldw(w_out, D, D, None, Wo, BWo)
            ldw(dr["w_up"][layer], D, DFF, gt, Wup, BWu)
            ldw(dr["w_down"][layer], DFF, D, None, Wdn, BWd)

            xt = [sbuf(st, f"xt{i}", [128, D], F32) for i in range(2)]
            Bxt = [P.buf(f"xt{i}", dma=True) for i in range(2)]
            yt = [sbuf(st, f"yt{i}", [128, 8, 128], BF16) for i in range(2)]
            Byt = [P.buf(f"yt{i}", dma=True) for i in range(2)]
            def pair(name, shape, dt):
                return [sbuf(st, f"{name}{i}", shape, dt) for i in range(2)], [P.buf(f"{name}{i}") for i in range(2)]
            x1_2, Bx1_2 = pair("x1", [128, D], F32)
            junk_2, Bjunk_2 = pair("junk", [128, D], BF16)
            ss_2, Bss_2 = pair("ss", [128, 1], F32)
            rstd_2, Brstd_2 = pair("rstd", [128, 1], F32)
            rstd2_2, Brstd2_2 = pair("rstd2", [128, 1], F32)
            xb_2, Bxb_2 = pair("xb", [128, D], BF16)
            xT_2, BxT_2 = pair("xT", [128, 8, 128], BF16)
            rl = [sbuf(st, f"rl{i}", [128, 512], F32) for i in range(2)]
            Brl = [P.buf(f"rl{i}") for i in range(2)]
            aT = sbuf(st, "aT", [128, 32, 128], BF16); BaT = P.buf("aT")
            xo = [sbuf(st, f"xo{i}", [128, D], F32) for i in range(2)]
            Bxo = [P.buf(f"xo{i}", dma=True) for i in range(2)]
            Bpo = [P.buf(f"po{i}") for i in range(2)]
            Btp = P.buf("tp")
            Bpu = [P.buf(f"pu{i}") for i in range(2)]
            Bpd = [P.buf(f"pd{i}") for i in range(2)]
            tpb = psum[2][:].bitcast(BF16)

            def front_a(t):
                s = t % 2
                x1, Bx1 = x1_2[s], Bx1_2[s]
                P.dma("sp", xt[s][:], xin[t * 128:(t + 1) * 128, :], Bxt[s], writes=[Bxt[s]])
                P.dma("sp", yt[s][:], yT[:, t * 128:(t + 1) * 128].rearrange("(c p) t -> p c t", p=128), Byt[s],
                      writes=[Byt[s]])
                for n in range(2):
                    for kc in range(8):
                        P.op("pe", "matmul", out=psum[n][:], lhsT=yt[s][:, kc, :], rhs=Wo[:, kc, n * 512:(n + 1) * 512],
                             start=(kc == 0), stop=(kc == 7), reads=[Byt[s], BWo], writes=[Bpo[n]])
                    P.op("dve", "tensor_tensor", out=x1[:, n * 512:(n + 1) * 512], in0=psum[n][:],
                         in1=xt[s][:, n * 512:(n + 1) * 512], op=ALU.add, reads=[Bpo[n], Bxt[s]], writes=[Bx1], nowaw=True)
                P.op("act", "activation", out=junk_2[s][:], in_=x1[:], func=AF.Square, accum_out=ss_2[s][:], reads=[Bx1],
                     writes=[Bss_2[s], Bjunk_2[s]])
                rstd_chain(ss_2[s][:], Bss_2[s], rstd_2[s][:], Brstd_2[s], 1, D)
                P.op("dve", "tensor_tensor", out=rstd2_2[s][:], in0=rstd_2[s][:], in1=rstd_2[s][:], op=ALU.mult,
                     reads=[Brstd_2[s]], writes=[Brstd2_2[s]])
                P.op("pool", "tensor_copy", out=xb_2[s][:], in_=x1[:], reads=[Bx1], writes=[Bxb_2[s]])

            def front_b(t):
                s = t % 2
                for kc in range(8):
                    P.op("pe", "transpose", out=tpb[:, kc * 128:(kc + 1) * 128], in_=xb_2[s][:, kc * 128:(kc + 1) * 128],
                         identity=idt[:], reads=[Bxb_2[s], Bid], writes=[Btp])
                P.op("act", "activation", out=xT_2[s][:].rearrange("p c t -> p (c t)"), in_=tpb[:, 0:1024], func=AF.Copy,
                     reads=[Btp], writes=[BxT_2[s]])

            def up(t):
                s = t % 2
                xT, BxT = xT_2[s], BxT_2[s]
                for fg in range(8):
                    b = fg % 2
                    for f4 in range(4):
                        fc = fg * 4 + f4
                        for kc in range(8):
                            P.op("pe", "matmul", out=psum[3 + b][:, f4 * 128:(f4 + 1) * 128],
                                 lhsT=Wup[:, kc, fc * 128:(fc + 1) * 128], rhs=xT[:, kc, :], start=(kc == 0 and f4 == 0),
                                 stop=(kc == 7), skip_group_check=True, reads=[BWu, BxT], writes=[Bpu[b]])
                    P.op("act", "activation", out=rl[b][:], in_=psum[3 + b][:], func=AF.Relu, reads=[Bpu[b]],
                         writes=[Brl[b]])
                    P.op("pool", "tensor_tensor", out=aT[:, fg * 4:(fg + 1) * 4, :].rearrange("p f t -> p (f t)"),
                         in0=rl[b][:], in1=rl[b][:], op=ALU.mult, reads=[Brl[b]], writes=[BaT], nowaw=True)

            def down(t):
                s = t % 2
                x1, Bx1 = x1_2[s], Bx1_2[s]
                for n in range(2):
                    for fc in range(32):
                        P.op("pe", "matmul", out=psum[5 + n][:], lhsT=aT[:, fc, :], rhs=Wdn[:, fc, n * 512:(n + 1) * 512],
                             start=(fc == 0), stop=(fc == 31), reads=[BaT, BWd], writes=[Bpd[n]])
                    P.op("dve", "scalar_tensor_tensor", out=xo[s][:, n * 512:(n + 1) * 512], in0=psum[5 + n][:],
                         scalar=rstd2_2[s][:], in1=x1[:, n * 512:(n + 1) * 512], op0=ALU.mult, op1=ALU.add,
                         reads=[Bpd[n], Brstd2_2[s], Bx1], writes=[Bxo[s]], nowaw=True)
                d = P.dma("pool", xout[t * 128:(t + 1) * 128, :], xo[s][:], Bxo[s], reads=[Bxo[s]])
                if layer == 1:
                    out_dmas.append(d)

            front_a(0)
            front_b(0)
            for t in range(T):
                if t + 1 < T:
                    front_a(t + 1)
                up(t)
                if t + 1 < T:
                    front_b(t + 1)
                down(t)
        P.barrier()

    if 1 in phases:
        qkv_phase(0)
    if 2 in phases:
        attn_phase(0)
    if 3 in phases:
        mlp_phase(0)
    if 4 in phases:
        qkv_phase(1)
    if 5 in phases:
        attn_phase(1)
    if 6 in phases:
        mlp_phase(1)
    stats = P.emit(final_waits=out_dmas[-2:] if out_dmas else [o for o in P.all if o.dsem is not None][-4:])
    return nc, stats


def make_in_maps(inputs, S):
    f = lambda a: np.ascontiguousarray(np.asarray(a, dtype=np.float32))
    x = f(inputs["x"])
    B = x.shape[0]
    shared = {}
    shared["g_mix_t"] = np.ascontiguousarray(f(inputs["g_mix"]).reshape(2, 8, 128).transpose(0, 2, 1))
    shared["g_mlp_t"] = np.ascontiguousarray(f(inputs["g_mlp"]).reshape(2, 8, 128).transpose(0, 2, 1))
    shared["w_in0"] = f(inputs["w_in_even"])[0]
    shared["w_out0"] = f(inputs["w_out_even"])[0]
    shared["w_in1"] = f(inputs["w_in_odd"])[0]
    shared["w_out1"] = f(inputs["w_out_odd"])[0]
    shared["w_up"] = f(inputs["w_up"])
    shared["w_down"] = f(inputs["w_down"])
    aq, ak = f(inputs["a_q_gain"])[0], f(inputs["a_k_gain"])[0]
    bq, bk = f(inputs["b_q_gain"])[0], f(inputs["b_k_gain"])[0]
    g0 = np.concatenate([np.tile(aq, 8), np.tile(ak, 8), np.tile(bq, 8), np.tile(bk, 2)])
    shared["gains0"] = np.ascontiguousarray(np.broadcast_to(g0[None, :], (128, g0.size)))
    cq, ck = f(inputs["c_q_gain"])[0], f(inputs["c_k_gain"])[0]
    g1 = np.concatenate([np.tile(cq, 16), np.tile(ck, 16)])
    shared["gains1"] = np.ascontiguousarray(np.broadcast_to(g1[None, :], (128, g1.size)))
    shared["sinks_r"] = np.ascontiguousarray(np.broadcast_to(f(inputs["b_sinks"])[0][None, :], (128, 8)))
    shared["bf_r"] = np.ascontiguousarray(np.broadcast_to(f(inputs["b_forget"])[0][None, :], (128, 16)))
    shared.update(_consts())
    maps = []
    for b in range(B):
        m = dict(shared)
        m["x"] = np.ascontiguousarray(x[b, :S])
        maps.append(m)
    return maps


_CACHE = {}


def kernel(**inputs):
    x = np.asarray(inputs["x"])
    B, S, _ = x.shape
    if S not in _CACHE:
        _CACHE[S] = build(S)[0]
    nc = _CACHE[S]
    maps = make_in_maps(inputs, S)
    res = run_bass_kernel_spmd(nc, maps, core_ids=list(range(B)))
    return np.stack([np.asarray(r["out"], dtype=np.float32) for r in res.results], axis=0)
```
